# Optimizing a Trainium2 kernel written in Bass

```python
import jax, jax.numpy as jnp
from jax import lax
import numpy as np

D_MODEL = 1024
BATCH = 8
SEQ = 2048
DEPTH = 2

HEAD_DIM = 64
D_MIX = D_MODEL
RG_WIDTH = 3 * D_MIX // 8
HG_WIDTH = 3 * D_MIX // 8
SC_WIDTH = D_MIX - RG_WIDTH - HG_WIDTH
RG_HEADS = RG_WIDTH // HEAD_DIM
HG_HEADS = HG_WIDTH // HEAD_DIM
RG_CONV = 4
SC_CONV = 3
FFN_CONV = 3
RG_C = 8.0
HG_CHUNK = 64
D_FF = 2816
PLE_DIM = 256
EPS = 1e-6
PROJ_SIZES = [RG_WIDTH, RG_WIDTH,
              HG_WIDTH, HG_WIDTH, HG_WIDTH, HG_WIDTH,
              SC_WIDTH, SC_WIDTH, SC_WIDTH]
N_PROJ = sum(PROJ_SIZES)
PROJ_SPLITS = [int(s) for s in np.cumsum(PROJ_SIZES)[:-1]]

kernel_name = 'hymba_style_rglru_hgrn2_shortconv_hybrid'


def rmsnorm(x, gain):
    xf = x.astype(jnp.float32)
    y = xf * lax.rsqrt(jnp.mean(xf * xf, axis=-1, keepdims=True) + EPS)
    return (y * gain.astype(jnp.float32)).astype(x.dtype)


def head_rmsnorm(y, gain):
    b, t, w = y.shape
    yh = y.reshape(b, t, w // HEAD_DIM, HEAD_DIM)
    yh = yh * lax.rsqrt(jnp.mean(yh * yh, axis=-1, keepdims=True) + EPS)
    return yh.reshape(b, t, w) * gain.astype(jnp.float32)


def causal_dwconv(x, w):
    k = w.shape[0]
    return lax.conv_general_dilated(
        x, w[:, None, :].astype(x.dtype), window_strides=(1,), padding=[(k - 1, 0)],
        dimension_numbers=('NWC', 'WIO', 'NWC'), feature_group_count=x.shape[-1])


def _linear_combine(c1, c2):
    a1, b1 = c1
    a2, b2 = c2
    return a1 * a2, a2 * b1 + b2


def rglru_group(u_in, gate_in, conv_w, conv_b, w_r, b_r, w_i, b_i, lam, gain):
    b, t, _ = u_in.shape
    u = (causal_dwconv(u_in, conv_w) + conv_b.astype(u_in.dtype)).astype(jnp.float32)
    uh = u.reshape(b, t, RG_HEADS, HEAD_DIM)
    r = jax.nn.sigmoid(jnp.einsum('btnd,nde->btne', uh, w_r.astype(jnp.float32))
                       + b_r.astype(jnp.float32)).reshape(b, t, RG_WIDTH)
    ig = jax.nn.sigmoid(jnp.einsum('btnd,nde->btne', uh, w_i.astype(jnp.float32))
                        + b_i.astype(jnp.float32)).reshape(b, t, RG_WIDTH)
    log_a = -RG_C * r * jax.nn.softplus(-lam.astype(jnp.float32))
    a = jnp.exp(log_a)
    beta = jnp.sqrt(-jnp.expm1(2.0 * log_a))
    _, h = lax.associative_scan(_linear_combine, (a, beta * ig * u), axis=1)
    return head_rmsnorm(h * jax.nn.gelu(gate_in.astype(jnp.float32)), gain)


def hgrn2_group(q, f_raw, v, g, lb, gain):
    b, t, _ = q.shape
    nc = t // HG_CHUNK
    qf = jax.nn.silu(q.astype(jnp.float32)) * HEAD_DIM ** -0.5
    f = lb + (1.0 - lb) * jax.nn.sigmoid(f_raw.astype(jnp.float32))
    log_f = jnp.log(f)
    k = 1.0 - f

    def to_chunks(u):
        return u.reshape(b, nc, HG_CHUNK, HG_HEADS, HEAD_DIM).transpose(1, 0, 3, 2, 4)

    mask = jnp.tril(jnp.ones((HG_CHUNK, HG_CHUNK), dtype=bool))[:, :, None]

    def step(S, inp):
        qc, kc, vc, lfc = inp
        G = jnp.cumsum(lfc, axis=2)
        o_inter = jnp.einsum('bhtk,bhkv->bhtv', qc * jnp.exp(G), S)
        diff = G[:, :, :, None, :] - G[:, :, None, :, :]
        decay = jnp.where(mask, jnp.exp(jnp.minimum(diff, 0.0)), 0.0)
        scores = jnp.einsum('bhtk,bhsk,bhtsk->bhts', qc, kc, decay)
        o_intra = jnp.einsum('bhts,bhsv->bhtv', scores, vc)
        G_last = G[:, :, -1:, :]
        S_new = (jnp.exp(G_last[:, :, 0, :])[..., None] * S
                 + jnp.einsum('bhsk,bhsv->bhkv', kc * jnp.exp(G_last - G), vc))
        return S_new, o_inter + o_intra

    S0 = jnp.zeros((b, HG_HEADS, HEAD_DIM, HEAD_DIM), jnp.float32)
    _, o = lax.scan(step, S0, (to_chunks(qf), to_chunks(k),
                               to_chunks(v.astype(jnp.float32)), to_chunks(log_f)))
    o = o.transpose(1, 0, 3, 2, 4).reshape(b, t, HG_WIDTH)
    return head_rmsnorm(o, gain) * jax.nn.silu(g.astype(jnp.float32))


def shortconv_group(bg, cg, v, conv_w, gain):
    y = bg * causal_dwconv(cg * v, conv_w)
    return head_rmsnorm(y.astype(jnp.float32), gain)


def conv_ffn(h, w_up, conv_w, w_down):
    gu = causal_dwconv(h @ w_up, conv_w)
    g, u = jnp.split(gu, 2, axis=-1)
    return (jax.nn.silu(g) * u) @ w_down


def setup_inputs(seed: int = 0) -> dict:
    key = jax.random.key(seed)
    ks = jax.random.split(key, 24)
    n = jax.random.normal
    f32 = jnp.float32
    u = jax.random.uniform(ks[10], (DEPTH, RG_WIDTH), f32, 0.9, 0.999)
    a0 = u ** (1.0 / RG_C)
    rg_lambda = jnp.log(a0) - jnp.log1p(-a0)
    return {
        'x': n(ks[0], (BATCH, SEQ, D_MODEL), f32),
        'p': n(ks[1], (DEPTH, BATCH, SEQ, PLE_DIM), f32),
        'norm_mix_gain': 1.0 + 0.02 * n(ks[2], (DEPTH, D_MODEL), f32),
        'w_in': n(ks[3], (DEPTH, D_MODEL, N_PROJ), f32) * D_MODEL ** -0.5,
        'rg_conv_w': n(ks[4], (DEPTH, RG_CONV, RG_WIDTH), f32) * RG_CONV ** -0.5,
        'rg_conv_b': 0.01 * n(ks[5], (DEPTH, RG_WIDTH), f32),
        'rg_w_r': n(ks[6], (DEPTH, RG_HEADS, HEAD_DIM, HEAD_DIM), f32) * HEAD_DIM ** -0.5,
        'rg_b_r': 0.01 * n(ks[7], (DEPTH, RG_HEADS, HEAD_DIM), f32),
        'rg_w_i': n(ks[8], (DEPTH, RG_HEADS, HEAD_DIM, HEAD_DIM), f32) * HEAD_DIM ** -0.5,
        'rg_b_i': 0.01 * n(ks[9], (DEPTH, RG_HEADS, HEAD_DIM), f32),
        'rg_lambda': rg_lambda,
        'hg_lower_bounds': 0.1 * n(ks[11], (DEPTH, HG_WIDTH), f32),
        'sc_conv_w': n(ks[12], (DEPTH, SC_CONV, SC_WIDTH), f32) * SC_CONV ** -0.5,
        'mix_out_gain': 1.0 + 0.02 * n(ks[13], (DEPTH, D_MIX), f32),
        'w_out': n(ks[14], (DEPTH, D_MIX, D_MODEL), f32) * D_MIX ** -0.5,
        'norm_ffn_gain': 1.0 + 0.02 * n(ks[15], (DEPTH, D_MODEL), f32),
        'ffn_w_up': n(ks[16], (DEPTH, D_MODEL, 2 * D_FF), f32) * D_MODEL ** -0.5,
        'ffn_conv_w': n(ks[17], (DEPTH, FFN_CONV, 2 * D_FF), f32) * FFN_CONV ** -0.5,
        'ffn_w_down': n(ks[18], (DEPTH, D_FF, D_MODEL), f32) * D_FF ** -0.5,
        'ple_w_proj': n(ks[19], (DEPTH, PLE_DIM, D_MODEL), f32) * PLE_DIM ** -0.5,
        'ple_w_gate': n(ks[20], (DEPTH, D_MODEL, D_MODEL), f32) * D_MODEL ** -0.5,
        'final_norm_gain': 1.0 + 0.02 * n(ks[21], (D_MODEL,), f32),
    }


def reference(x, p, norm_mix_gain, w_in, rg_conv_w, rg_conv_b, rg_w_r, rg_b_r, rg_w_i, rg_b_i,
              rg_lambda, hg_lower_bounds, sc_conv_w, mix_out_gain, w_out, norm_ffn_gain,
              ffn_w_up, ffn_conv_w, ffn_w_down, ple_w_proj, ple_w_gate, final_norm_gain):
    dt = x.dtype
    sm = jax.nn.softmax(hg_lower_bounds.astype(jnp.float32), axis=0)
    lower_bound = jnp.cumsum(sm, axis=0) - sm[0:1]
    for i in range(DEPTH):
        h = rmsnorm(x, norm_mix_gain[i])
        z = h @ w_in[i]
        a_u, a_gate, b_q, b_f, b_v, b_g, c_b, c_c, c_v = jnp.split(z, PROJ_SPLITS, axis=-1)
        g_a, g_b, g_c = jnp.split(mix_out_gain[i], [RG_WIDTH, RG_WIDTH + HG_WIDTH])
        y_a = rglru_group(a_u, a_gate, rg_conv_w[i], rg_conv_b[i], rg_w_r[i], rg_b_r[i],
                          rg_w_i[i], rg_b_i[i], rg_lambda[i], g_a)
        y_b = hgrn2_group(b_q, b_f, b_v, b_g, lower_bound[i], g_b)
        y_c = shortconv_group(c_b, c_c, c_v, sc_conv_w[i], g_c)
        y = jnp.concatenate([y_a, y_b, y_c], axis=-1).astype(dt)
        x = x + y @ w_out[i]
        x = x + conv_ffn(rmsnorm(x, norm_ffn_gain[i]), ffn_w_up[i], ffn_conv_w[i], ffn_w_down[i])
        gate = jax.nn.sigmoid((x @ ple_w_gate[i]).astype(jnp.float32))
        x = x + (gate * (p[i] @ ple_w_proj[i]).astype(jnp.float32)).astype(dt)
    return rmsnorm(x, final_norm_gain)
```

```python
from contextlib import ExitStack
import numpy as np
import concourse.bass as bass
import concourse.mybir as mybir
from concourse.bass_utils import run_bass_kernel_spmd

F32 = mybir.dt.float32
BF16 = mybir.dt.bfloat16
AF = mybir.ActivationFunctionType
ALU = mybir.AluOpType

D = 1024
T = 2048
TT = 512
NT = T // TT
DEPTH = 2
DFF = 2816
NPROJ = 3072
PLE = 256
EPS = 1e-6
KC = D // 128
FC = DFF // 128
WSLOT = 2048

_PAR = {}
_off = 0
for _n, _c in [("g_mix", 8), ("g_ffn", 8), ("g_out", 8), ("rg_cw", 12), ("rg_cb", 3), ("rg_br", 3),
               ("rg_bi", 3), ("rg_lam", 3), ("hg_lb0", 3), ("hg_lb1", 3), ("sc_cw", 6),
               ("ffn_cw", 132), ("g_fin", 8)]:
    _PAR[_n] = (_off, _c)
    _off += _c
NPAR = _off


class Buf:
    __slots__ = ("name", "w", "r")

    def __init__(self, name):
        self.name = name
        self.w = None
        self.r = {}


class Sched:
    def __init__(self, nc, es):
        self.nc = nc
        self.es = es
        self.engs = {"pe": nc.tensor, "act": nc.scalar, "dve": nc.vector, "pool": nc.gpsimd, "sp": nc.sync}
        self.sems = {}
        self.cnt = {}
        self.seen = {e: {} for e in self.engs}
        self.snaps = {}
        for e in self.engs:
            self.newsem(e)

    def newsem(self, key):
        self.sems[key] = self.es.enter_context(self.nc.semaphore("s_" + str(key).replace(" ", "")))
        self.cnt[key] = 0

    def _waits(self, eng, r, w):
        need = {}

        def req(dep, raw):
            if dep is None:
                return
            key, c = dep
            if key == eng and eng == "pe":
                return
            if need.get(key, 0) < c:
                need[key] = c

        for b in r:
            req(b.w, True)
        for b in w:
            req(b.w, False)
            for key, c in b.r.items():
                req((key, c), False)
        E = self.engs[eng]
        seen = self.seen[eng]
        for key, c in need.items():
            if seen.get(key, 0) < c:
                E.wait_ge(self.sems[key], c)
                seen[key] = c
                snap = self.snaps.get((key, c))
                if snap:
                    for k2, v2 in snap.items():
                        if k2 != eng and seen.get(k2, 0) < v2:
                            seen[k2] = v2

    def _record(self, key, c, r, w):
        for b in r:
            if b.r.get(key, 0) < c:
                b.r[key] = c
        for b in w:
            b.w = (key, c)
            b.r = {}

    def op(self, eng, emit, r=(), w=(), signal=True):
        self._waits(eng, r, w)
        ins = emit(self.engs[eng])
        if signal:
            self.cnt[eng] += 1
            c = self.cnt[eng]
            ins.then_inc(self.sems[eng], 1)
            self.snaps[(eng, c)] = dict(self.seen[eng])
        else:
            c = self.cnt[eng] + 1
        self._record(eng, c, r, w)

    def dma(self, semkey, emit, r=(), w=(), q="sp"):
        self._waits(q, r, w)
        ins = emit(self.engs[q])
        self.cnt[semkey] += 16
        c = self.cnt[semkey]
        ins.then_inc(self.sems[semkey], 16)
        self.snaps[(semkey, c)] = dict(self.seen[q])
        self._record(semkey, c, r, w)


class Pool_:
    def __init__(self, nc, es, name, n, shape, dtype):
        self.tiles = [es.enter_context(nc.sbuf_tensor(f"sb_{name}{i}", shape, dtype)) for i in range(n)]
        self.bufs = [Buf(f"{name}{i}") for i in range(n)]
        self.i = 0

    def get(self):
        i = self.i
        self.i = (i + 1) % len(self.tiles)
        return self.tiles[i], self.bufs[i]


def G_IN(g): return g
def G_OUT(g): return 12 + g
def G_UP(j): return 16 + j
def G_DN(n, hf): return 38 + 2 * n + hf
def G_PG(g): return 54 + g
G_PP = 58
G_GATES = 59
NGRP = 60


def _build(order, n_layers=DEPTH, final_norm=True):
    record = order is None
    rec = []
    nc = bass.Bass("TRN2", target_bir_lowering=False)
    es = ExitStack()
    x_d = nc.dram_tensor("x", [T, D], F32, kind="ExternalInput").ap()
    p_d = nc.dram_tensor("p", [DEPTH, T, PLE], F32, kind="ExternalInput").ap()
    par_d = nc.dram_tensor("par", [DEPTH, 128, NPAR], F32, kind="ExternalInput").ap()
    cst_d = nc.dram_tensor("cst", [128, 128 * 3 + 512], F32, kind="ExternalInput").ap()
    w_in_d = nc.dram_tensor("w_in", [DEPTH, D, NPROJ], F32, kind="ExternalInput").ap()
    w_g_d = nc.dram_tensor("w_gates", [DEPTH, 128, 768], F32, kind="ExternalInput").ap()
    w_out_d = nc.dram_tensor("w_out", [DEPTH, D, D], F32, kind="ExternalInput").ap()
    w_up_d = nc.dram_tensor("w_up", [DEPTH, D, 2 * DFF], F32, kind="ExternalInput").ap()
    w_dn_d = nc.dram_tensor("w_down", [DEPTH, DFF, D], F32, kind="ExternalInput").ap()
    w_pg_d = nc.dram_tensor("w_pgate", [DEPTH, D, D], F32, kind="ExternalInput").ap()
    w_pp_d = nc.dram_tensor("w_pproj", [DEPTH, PLE, D], F32, kind="ExternalInput").ap()
    out_d = nc.dram_tensor("out", [T, D], F32, kind="ExternalOutput").ap()
    wscr = nc.dram_tensor("wscr", [n_layers, NGRP, 128, WSLOT], BF16, kind="Internal").ap()

    S = Sched(nc, es)

    def sb(name, shape, dt):
        return es.enter_context(nc.sbuf_tensor("sb_" + name, shape, dt))

    def ps(name, shape, dt=F32):
        return es.enter_context(nc.psum_tensor("ps_" + name, shape, dt))

    x_res = sb("x_res", [128, KC, T], F32)
    xb_ = [[Buf(f"x{k}_{i}") for i in range(NT)] for k in range(KC)]
    par = [sb(f"par{l}", [128, NPAR], F32) for l in range(DEPTH)]
    par_b = [Buf(f"par{l}") for l in range(DEPTH)]
    der = [sb(f"der{l}", [128, 16], F32) for l in range(DEPTH)]
    der_b = [Buf(f"der{l}") for l in range(DEPTH)]
    cst = sb("cst", [128, 128 * 3 + 512], F32)
    cst_b = Buf("cst")
    cbf = sb("cbf", [128, 128 * 3], BF16)
    cbf_b = Buf("cbf")
    onesD = sb("onesD", [128, 128], BF16)
    eps_t = sb("eps_t", [128, 1], F32)
    ident_f = cst[:, 0:128]
    smask = cst[:, 384:896]
    ident_b = cbf[:, 0:128]
    ones64_b = cbf[:, 128:256]
    mask_b = cbf[:, 256:384]

    CU = 1024
    NSTG, NCVT, NW = 2, 2, 4
    wst = [sb(f"wst{i}", [128, CU], F32) for i in range(NSTG)]
    wst_b = [Buf(f"wst{i}") for i in range(NSTG)]
    cvt = [sb(f"cvt{i}", [128, CU], BF16) for i in range(NCVT)]
    cvt_b = [Buf(f"cvt{i}") for i in range(NCVT)]
    wbf = [sb(f"wbf{i}", [128, WSLOT], BF16) for i in range(NW)]
    wbf_b = [Buf(f"wbf{i}") for i in range(NW)]
    wgt = sb("wgt", [128, 768], BF16)
    wgt_b = Buf("wgt")
    for i in range(NSTG):
        S.newsem(("cvi", i))
    for i in range(NCVT):
        S.newsem(("cvo", i))
    for i in range(NW):
        S.newsem(("ws", i))
    for k in ("io0", "io1", "misc", "outd0", "outd1", "wg"):
        S.newsem(k)

    hyM = sb("hyM", [128, KC, TT], BF16)
    hyM_b = [Buf(f"hyM{k}") for k in range(KC)]
    hyF = sb("hyF", [128, KC, TT], BF16)
    hyF_b = [Buf(f"hyF{k}") for k in range(KC)]
    au = sb("au", [128, 3, TT + 3], BF16)
    au_b = [Buf(f"au{c}") for c in range(3)]
    gg = sb("gg", [128, 3, TT], BF16)
    gg_b = [Buf(f"gg{c}") for c in range(3)]
    sq = sb("sq", [128, 3, TT], BF16)
    sq_b = [Buf(f"sq{c}") for c in range(3)]
    sg = sb("sg", [128, 3, TT], F32)
    sg_b = [Buf(f"sg{c}") for c in range(3)]
    vtok = sb("vtok", [128, 4, 384], BF16)
    vtok_b = [Buf(f"vtok{j}") for j in range(4)]
    sgg = sb("sgg", [128, 3, TT], BF16)
    sgg_b = [Buf(f"sgg{c}") for c in range(3)]
    cb = sb("cb", [128, 2, TT], BF16)
    cb_b = [Buf(f"cb{c}") for c in range(2)]
    ccv = sb("ccv", [128, 2, TT + 2], BF16)
    ccv_b = [Buf(f"ccv{c}") for c in range(2)]
    qgp = sb("qgp", [128, 3, TT], BF16)
    qgp_b = [Buf(f"qgp{c}") for c in range(3)]
    eGl = sb("eGl", [128, 3, 8], F32)
    eGl_b = [Buf(f"eGl{c}") for c in range(3)]
    a_buf = sb("a_buf", [128, FC, TT], BF16)
    a_b = [Buf(f"a{j}") for j in range(FC)]
    fcar = sb("fcar", [128, 2 * FC, 2], BF16)
    fcar_b = [Buf(f"fcar{j}") for j in range(2 * FC)]
    io = [sb(f"io{i}", [128, 1024], F32) for i in range(2)]
    io_b = [Buf(f"io{i}") for i in range(2)]
    pT = sb("pT", [128, 2, TT], BF16)
    pT_b = [Buf("pT0"), Buf("pT1")]
    hst = sb("hst", [128, 3], F32)
    hst_b = [Buf(f"hst{c}") for c in range(3)]
    Sst = sb("Sst", [128, 3, 64], F32)
    Sbf = sb("Sbf", [128, 3, 64], BF16)
    Sst_b = [Buf(f"Sst{c}") for c in range(3)]
    Sbf_b = [Buf(f"Sbf{c}") for c in range(3)]
    kdT = sb("kdT", [128, 3, 4, 128], BF16)
    kdT_b = [Buf(f"kdT{c}") for c in range(3)]
    Pm = sb("Pm", [128, 2, 4, 128], BF16)
    Pm_b = [Buf("Pm0"), Buf("Pm1")]
    t32M = Pool_(nc, es, "t32M_", 5, [128, TT], F32)
    t16M = Pool_(nc, es, "t16M_", 5, [128, TT + 2], BF16)
    t32F = Pool_(nc, es, "t32F_", 2, [128, TT], F32)
    t16F = Pool_(nc, es, "t16F_", 6, [128, TT + 2], BF16)

    class Banks:
        def __init__(self, name, n):
            self.t = [ps(f"{name}{i}", [128, 512]) for i in range(n)]
            self.b = [Buf(f"{name}{i}") for i in range(n)]
            self.i = 0

        def get(self):
            i = self.i
            self.i = (i + 1) % len(self.t)
            return self.t[i], self.b[i]

    accM = Banks("accM", 3)
    accF = Banks("accF", 2)
    po = [ps(f"po{i}", [128, 512]) for i in range(3)]
    po_b = [Buf(f"po{i}") for i in range(3)]

    def pc(l, name, j=0, n=1):
        o, _ = _PAR[name]
        return par[l][:, o + j:o + j + n]

    def mm(out, lhsT, rhs, start, stop, r, w, signal, skip=False):
        S.op("pe", lambda e: e.matmul(out, lhsT, rhs, start=start, stop=stop, skip_group_check=skip),
             r=r, w=w, signal=signal)

    S.dma("misc", lambda e: e.dma_start(out=cst[:], in_=cst_d), w=[cst_b])
    for l in range(DEPTH):
        S.dma("misc", lambda e, l=l: e.dma_start(out=par[l][:], in_=par_d[l]), w=[par_b[l]])
    for b_ in [cst_b] + par_b:
        b_.w = ("misc", S.cnt["misc"])
    S.op("dve", lambda e: e.tensor_copy(out=cbf[:], in_=cst[:, 0:384]), r=[cst_b], w=[cbf_b])
    S.op("dve", lambda e: e.memset(onesD[:], 1.0 / D), w=[cbf_b])
    S.op("dve", lambda e: e.memset(eps_t[:], EPS), w=[cbf_b])
    for l in range(DEPTH):
        dd = der[l]
        lam = pc(l, "rg_lam", 0, 3)
        S.op("act", lambda e, dd=dd, lam=lam: e.activation(out=dd[:, 0:3], in_=lam, func=AF.Exp, scale=-1.0),
             r=[par_b[l]], w=[der_b[l]])
        S.op("act", lambda e, dd=dd: e.activation(out=dd[:, 0:3], in_=dd[:, 0:3], func=AF.Ln, bias=1.0),
             r=[der_b[l]], w=[der_b[l]])
        S.op("dve", lambda e, dd=dd: e.tensor_scalar(out=dd[:, 3:6], in0=dd[:, 0:3], scalar1=-16.0, scalar2=None,
                                                     op0=ALU.mult), r=[der_b[l]], w=[der_b[l]])
        S.op("dve", lambda e, dd=dd: e.tensor_scalar(out=dd[:, 0:3], in0=dd[:, 0:3], scalar1=-8.0, scalar2=None,
                                                     op0=ALU.mult), r=[der_b[l]], w=[der_b[l]])
        if l == 0:
            S.op("dve", lambda e, dd=dd: e.memset(dd[:, 6:9], 0.0), w=[der_b[l]])
        else:
            S.op("dve", lambda e, dd=dd, l=l: e.tensor_tensor(out=dd[:, 6:9], in0=pc(l, "hg_lb1", 0, 3),
                                                              in1=pc(l, "hg_lb0", 0, 3), op=ALU.subtract),
                 r=[par_b[l]], w=[der_b[l]])
            S.op("act", lambda e, dd=dd: e.activation(out=dd[:, 6:9], in_=dd[:, 6:9], func=AF.Sigmoid),
                 r=[der_b[l]], w=[der_b[l]])
        S.op("dve", lambda e, dd=dd: e.tensor_scalar(out=dd[:, 9:12], in0=dd[:, 6:9], scalar1=-1.0, scalar2=1.0,
                                                     op0=ALU.mult, op1=ALU.add), r=[der_b[l]], w=[der_b[l]])
        S.op("dve", lambda e, dd=dd: e.tensor_scalar(out=dd[:, 12:15], in0=dd[:, 9:12], scalar1=-1.0, scalar2=None,
                                                     op0=ALU.mult), r=[der_b[l]], w=[der_b[l]])

    def group_units(l, g):
        def kp(ap_, k):
            return ap_.rearrange("(k p) n -> p k n", p=128), k
        if g < 12:
            c0 = 256 * g
            return [(w_in_d[l][512 * h:512 * h + 512, c0:c0 + 256], 4, 1024 * h, 1024) for h in range(2)]
        if g < 16:
            c0 = 256 * (g - 12)
            return [(w_out_d[l][512 * h:512 * h + 512, c0:c0 + 256], 4, 1024 * h, 1024) for h in range(2)]
        if g < 38:
            j = g - 16
            return [(w_up_d[l][:, h * DFF + 128 * j:h * DFF + 128 * j + 128], 8, 1024 * h, 1024) for h in range(2)]
        if g < 54:
            n, hf = (g - 38) // 2, (g - 38) % 2
            r0 = 1408 * hf
            return [(w_dn_d[l][r0:r0 + 1024, 128 * n:128 * n + 128], 8, 0, 1024),
                    (w_dn_d[l][r0 + 1024:r0 + 1408, 128 * n:128 * n + 128], 3, 1024, 384)]
        if g < 58:
            c0 = 256 * (g - 54)
            return [(w_pg_d[l][512 * h:512 * h + 512, c0:c0 + 256], 4, 1024 * h, 1024) for h in range(2)]
        if g == G_PP:
            return [(w_pp_d[l][128 * h:128 * h + 128, :], None, 1024 * h, 1024) for h in range(2)]
        return [(w_g_d[l], None, 0, 768)]

    GN = [2048] * 38 + [1408] * 16 + [2048] * 5 + [768]
    scr_b = [[[Buf(f"scr{l}_{g}_{u}") for u in range(2)] for g in range(NGRP)] for l in range(n_layers)]
    conv_order = list(range(12)) + [G_GATES] + list(range(12, 59))
    cvs = {"n": [0] * n_layers, "tot": 0}
    S.newsem(("cvj", 0))
    S.newsem(("cvj", 1))
    stg_slots = [(wst[0], wst_b[0], ("cvi", 0)), (wst[1], wst_b[1], ("cvi", 1))]
    stg_slots0 = [(wst[0], wst_b[0], ("cvi", 0)), (wst[1], wst_b[1], ("cvi", 1)),
                  (io[0], io_b[0], "io0"), (io[1], io_b[1], "io1")]
    cvt_slots = [(cvt[0][:, :], [cvt_b[0]], ("cvo", 0)), (cvt[1][:, :], [cvt_b[1]], ("cvo", 1))]
    cvt_slots0 = list(cvt_slots)
    for m in range(6, 11):
        S.newsem(("cvo", m))
        cvt_slots0.append((a_buf[:, 2 * m:2 * m + 2, :].rearrange("p a t -> p (a t)"), [a_b[2 * m], a_b[2 * m + 1]], ("cvo", m)))

    def conv_emit(l):
        idx = cvs["n"][l]
        if idx >= NGRP:
            return False
        cvs["n"][l] += 1
        g = conv_order[idx]
        units = group_units(l, g)
        for ui, (src, k, off, n) in enumerate(units):
            t = cvs["tot"]
            cvs["tot"] += 1
            sl_ = stg_slots0 if l == 0 else stg_slots
            st_t, st_b, st_sem = sl_[t % len(sl_)]
            cl_ = cvt_slots0 if (l == 0 and idx < 17) else cvt_slots
            cv_t, cv_b, cv_sem = cl_[t % len(cl_)]
            inq = "sp"
            if k is None:
                S.dma(st_sem, lambda e, st_t=st_t, src=src, n=n: e.dma_start(out=st_t[:, 0:n], in_=src),
                      w=[st_b], q=inq)
            else:
                S.dma(st_sem, lambda e, st_t=st_t, src=src, n=n, k=k: e.dma_start(
                    out=st_t[:, 0:n].rearrange("p (k n) -> p k n", k=k),
                    in_=src.rearrange("(k p) n -> p k n", p=128)), w=[st_b], q=inq)
            ce = ("act", "dve", "pool", "act", "dve")[t % 5] if l == 0 else ("pool", "pool", "act")[t % 3]
            if ce == "act":
                S.op("act", lambda e, st_t=st_t, cv_t=cv_t, n=n: e.activation(out=cv_t[:, 0:n], in_=st_t[:, 0:n], func=AF.Copy),
                     r=[st_b], w=cv_b)
            else:
                S.op(ce, lambda e, st_t=st_t, cv_t=cv_t, n=n: e.tensor_copy(out=cv_t[:, 0:n], in_=st_t[:, 0:n]),
                     r=[st_b], w=cv_b)
            S.dma(cv_sem, lambda e, cv_t=cv_t, off=off, n=n, g=g: e.dma_start(out=wscr[l, g, :, off:off + n],
                                                                            in_=cv_t[:, 0:n]),
                  r=cv_b, w=[scr_b[l][g][ui]], q="pool")
        return True

    def conv_upto(l, g):
        pos = conv_order.index(g)
        while cvs["n"][l] <= pos:
            conv_emit(l)

    ws = {"issued": 0, "next": 0}
    stream = order if not record else None

    def issue(l, g, qi):
        conv_upto(l, min(g, 58) if l > 0 else g)
        if l == 0 and g < 58:
            conv_upto(0, min(g + 5, 58))
        if l > 0:
            while conv_emit(l):
                pass
        sl = qi % NW
        n = GN[g]
        S.dma(("ws", sl), lambda e: e.dma_start(out=wbf[sl][:, 0:n], in_=wscr[l, g, :, 0:n]),
              r=scr_b[l][g], w=[wbf_b[sl]])

    def w_next(l, g):
        qi = ws["next"]
        ws["next"] += 1
        if record:
            rec.append((l, g))
            issue(l, g, qi)
        else:
            assert stream[qi] == (l, g), (qi, stream[qi], (l, g))
            while ws["issued"] <= min(qi + NW - 1, len(stream) - 1):
                q2 = ws["issued"]
                l2, g2 = stream[q2]
                issue(l2, g2, q2)
                if l2 == 0 and n_layers > 1 and cvs["n"][0] >= NGRP and q2 >= 59 and q2 % 2 == 0:
                    conv_emit(1)
                ws["issued"] += 1
        sl = qi % NW
        return wbf[sl], wbf_b[sl]

    def load_gates(l):
        conv_upto(l, G_GATES)
        S.dma("wg", lambda e: e.dma_start(out=wgt[:], in_=wscr[l, G_GATES, :, 0:768]), r=scr_b[l][G_GATES], w=[wgt_b])

    def rstd_from_psum(pst, pst_b, t32):
        tl, tlb = t32.get()
        S.op("act", lambda e: e.activation(out=tl[:], in_=pst[:], func=AF.Ln, bias=eps_t[:, 0:1], scale=1.0),
             r=[pst_b, cbf_b], w=[tlb])
        S.op("act", lambda e: e.activation(out=tl[:], in_=tl[:], func=AF.Exp, scale=-0.5), r=[tlb], w=[tlb])
        return tl, tlb

    def rmsnorm(l, i, gname, hy, hy_b, acc, t32, t16):
        pst, pst_b = acc.get()
        for k in range(KC):
            tq, tqb = t16.get()
            xs = x_res[:, k, TT * i:TT * i + TT]
            S.op("act", lambda e, tq=tq, xs=xs: e.activation(out=tq[:, 0:TT], in_=xs, func=AF.Square),
                 r=[xb_[k][i]], w=[tqb])
            mm(pst[:], onesD[:], tq[:, 0:TT], k == 0, k == KC - 1, [tqb, cbf_b], [pst_b], True)
        rs, rsb = rstd_from_psum(pst, pst_b, t32)
        for k in range(KC):
            xs = x_res[:, k, TT * i:TT * i + TT]
            S.op("dve", lambda e, k=k, xs=xs: e.scalar_tensor_tensor(out=hy[:, k, :], in0=xs, scalar=pc(l, gname, k),
                                                                     in1=rs[:], op0=ALU.mult, op1=ALU.mult),
                 r=[xb_[k][i], rsb, par_b[l]], w=[hy_b[k]])

    def headnorm_sq(src, srcb):
        tq, tqb = t16M.get()
        S.op("act", lambda e: e.activation(out=tq[:, 0:TT], in_=src[:], func=AF.Square), r=[srcb], w=[tqb])
        return tq, tqb

    def headnorm_to_y(l, src, srcb, ychunk, tq, tqb):
        pst, pst_b = accM.get()
        mm(pst[:], ones64_b, tq[:, 0:TT], True, True, [tqb, cbf_b], [pst_b], True)
        rs, rsb = rstd_from_psum(pst, pst_b, t32M)
        yield
        S.op("dve", lambda e: e.scalar_tensor_tensor(out=hyM[:, ychunk, :], in0=src[:], scalar=pc(l, "g_out", ychunk),
                                                     in1=rs[:], op0=ALU.mult, op1=ALU.mult),
             r=[srcb, rsb, par_b[l]], w=[hyM_b[ychunk]])

    def load_x(tbs):
        for tb in tbs:
            s = tb % 2
            S.dma(f"io{s}", lambda e, s=s, tb=tb: e.dma_start(out=io[s][:], in_=x_d[128 * tb:128 * tb + 128, :]),
                  w=[io_b[s]])
            i = tb // 4
            for half in range(2):
                pst, pst_b = accM.get()
                for kk in range(4):
                    k = 4 * half + kk
                    S.op("pe", lambda e, pst=pst, kk=kk, k=k, s=s: e.transpose(pst[:, 128 * kk:128 * kk + 128],
                                                                             io[s][:, 128 * k:128 * k + 128], ident_f),
                         r=[io_b[s], cst_b], w=[pst_b], signal=(kk == 3))
                dst = x_res[:, 4 * half:4 * half + 4, 128 * tb:128 * tb + 128]
                S.op("act" if half == 0 else "dve",
                     (lambda e, dst=dst, pst=pst: e.activation(out=dst, in_=pst[:].rearrange("p (k t) -> p k t", k=4),
                                                               func=AF.Copy)) if half == 0 else
                     (lambda e, dst=dst, pst=pst: e.tensor_copy(out=dst, in_=pst[:].rearrange("p (k t) -> p k t", k=4))),
                     r=[pst_b], w=[xb_[4 * half + kk][i] for kk in range(4)])
            yield

    for _ in load_x(range(0, 4)):
        pass

    def stage_M(l, i):
        dd = der[l]
        tsl = slice(TT * i, TT * i + TT)
        hy, hy_b = hyM, hyM_b
        if i == 0:
            load_gates(l)
            for c in range(3):
                S.op("dve", lambda e, c=c: e.memset(au[:, c, 0:3], 0.0), w=[au_b[c]])
                S.op("dve", lambda e, c=c: e.memset(hst[:, c:c + 1], 0.0), w=[hst_b[c]])
                S.op("dve", lambda e, c=c: e.memset(Sst[:, c, :], 0.0), w=[Sst_b[c]])
                S.op("dve", lambda e, c=c: e.memset(Sbf[:, c, :], 0.0), w=[Sbf_b[c]])
            for c in range(2):
                S.op("dve", lambda e, c=c: e.memset(ccv[:, c, 0:2], 0.0), w=[ccv_b[c]])
        rmsnorm(l, i, "g_mix", hy, hy_b, accM, t32M, t16M)
        yield
        cc_tmp = {}
        wt, wtb = None, None
        for n in range(24):
            if n % 2 == 0:
                wt, wtb = w_next(l, G_IN(n // 2))
            wv = wt[:, 0:2048].rearrange("p (k n) -> p k n", k=8)
            nn = n % 2
            if 12 <= n < 15:
                c = n - 12
                pst, pst_b = accM.get()
                for jb in range(4):
                    for k in range(KC):
                        mm(pst[:, 128 * jb:128 * jb + 128], hy[:, k, 128 * jb:128 * jb + 128],
                           wv[:, k, 128 * nn:128 * nn + 128], k == 0, k == KC - 1,
                           [hy_b[k], wtb], [pst_b], (k == KC - 1 and jb == 3))
                S.op("act", lambda e, pst=pst, c=c: e.activation(
                    out=vtok[:, :, 128 * c:128 * c + 128], in_=pst[:].rearrange("p (j v) -> p j v", j=4),
                    func=AF.Copy), r=[pst_b], w=vtok_b)
            else:
                pst, pst_b = accM.get()
                for k in range(KC):
                    mm(pst[:], wv[:, k, 128 * nn:128 * nn + 128], hy[:, k, :], k == 0, k == KC - 1,
                       [hy_b[k], wtb], [pst_b], k == KC - 1)
                if n < 3:
                    S.op("act", lambda e, pst=pst, n=n: e.activation(out=au[:, n, 3:TT + 3], in_=pst[:], func=AF.Copy),
                         r=[pst_b], w=[au_b[n]])
                elif n < 6:
                    S.op("act", lambda e, pst=pst, n=n: e.activation(out=gg[:, n - 3, :], in_=pst[:],
                                                                     func=AF.Gelu_apprx_tanh), r=[pst_b], w=[gg_b[n - 3]])
                elif n < 9:
                    S.op("act", lambda e, pst=pst, n=n: e.activation(out=sq[:, n - 6, :], in_=pst[:], func=AF.Silu),
                         r=[pst_b], w=[sq_b[n - 6]])
                elif n < 12:
                    S.op("act", lambda e, pst=pst, n=n: e.activation(out=sg[:, n - 9, :], in_=pst[:], func=AF.Sigmoid),
                         r=[pst_b], w=[sg_b[n - 9]])
                elif n < 18:
                    S.op("act", lambda e, pst=pst, n=n: e.activation(out=sgg[:, n - 15, :], in_=pst[:], func=AF.Silu),
                         r=[pst_b], w=[sgg_b[n - 15]])
                elif n < 20:
                    S.op("act", lambda e, pst=pst, n=n: e.activation(out=cb[:, n - 18, :], in_=pst[:], func=AF.Copy),
                         r=[pst_b], w=[cb_b[n - 18]])
                elif n < 22:
                    cct, cctb = t32M.get()
                    cc_tmp[n - 20] = (cct, cctb)
                    S.op("act", lambda e, pst=pst, cct=cct: e.activation(out=cct[:], in_=pst[:], func=AF.Copy),
                         r=[pst_b], w=[cctb])
                else:
                    c = n - 22
                    cct, cctb = cc_tmp[c]
                    S.op("dve", lambda e, pst=pst, c=c, cct=cct: e.tensor_tensor(out=ccv[:, c, 2:TT + 2], in0=cct[:],
                                                                                 in1=pst[:], op=ALU.mult),
                         r=[pst_b, cctb], w=[ccv_b[c]])
            if n % 2 == 1 and n < 20:
                yield

        for c in range(2):
            u, ub_ = t32M.get()
            S.op("dve", lambda e, c=c, u=u: e.tensor_scalar(out=u[:], in0=ccv[:, c, 2:TT + 2],
                                                             scalar1=pc(l, "sc_cw", 2 * 2 + c), scalar2=None,
                                                             op0=ALU.mult), r=[ccv_b[c], par_b[l]], w=[ub_])
            for tap in (1, 0):
                S.op("dve", lambda e, c=c, u=u, tap=tap: e.scalar_tensor_tensor(
                    out=u[:], in0=ccv[:, c, tap:tap + TT], scalar=pc(l, "sc_cw", 2 * tap + c), in1=u[:],
                    op0=ALU.mult, op1=ALU.add), r=[ccv_b[c], ub_, par_b[l]], w=[ub_])
            S.op("act", lambda e, c=c: e.activation(out=ccv[:, c, 0:2], in_=ccv[:, c, TT:TT + 2], func=AF.Copy),
                 r=[ccv_b[c]], w=[ccv_b[c]])
            S.op("dve", lambda e, u=u, c=c: e.tensor_tensor(out=u[:], in0=u[:], in1=cb[:, c, :], op=ALU.mult),
                 r=[ub_, cb_b[c]], w=[ub_])
            yield
            tq, tqb = headnorm_sq(u, ub_)
            yield
            yield from headnorm_to_y(l, u, ub_, 6 + c, tq, tqb)
            yield

        for c in range(3):
            u, ub_ = t32M.get()
            S.op("dve", lambda e, c=c, u=u: e.tensor_scalar(out=u[:], in0=au[:, c, 3:TT + 3],
                                                             scalar1=pc(l, "rg_cw", 3 * 3 + c), scalar2=pc(l, "rg_cb", c),
                                                             op0=ALU.mult, op1=ALU.add),
                 r=[au_b[c], par_b[l]], w=[ub_])
            for tap in (2, 1, 0):
                S.op("dve", lambda e, c=c, u=u, tap=tap: e.scalar_tensor_tensor(
                    out=u[:], in0=au[:, c, tap:tap + TT], scalar=pc(l, "rg_cw", 3 * tap + c), in1=u[:],
                    op0=ALU.mult, op1=ALU.add), r=[au_b[c], ub_, par_b[l]], w=[ub_])
            yield
            S.op("act", lambda e, c=c: e.activation(out=au[:, c, 0:3], in_=au[:, c, TT:TT + 3], func=AF.Copy),
                 r=[au_b[c]], w=[au_b[c]])
            u16, u16b = t16M.get()
            S.op("act", lambda e, u=u, u16=u16: e.activation(out=u16[:, 0:TT], in_=u[:], func=AF.Copy),
                 r=[ub_], w=[u16b])
            yield
            gates = []
            for gi, bname in ((0, "rg_br"), (1, "rg_bi")):
                pst, pst_b = accM.get()
                mm(pst[:], wgt[:, 384 * gi + 128 * c:384 * gi + 128 * c + 128], u16[:, 0:TT], True, True,
                   [u16b, wgt_b], [pst_b], True)
                gt, gtb = t32M.get()
                S.op("act", lambda e, pst=pst, gt=gt, bname=bname, c=c: e.activation(
                    out=gt[:], in_=pst[:], func=AF.Sigmoid, bias=pc(l, bname, c)),
                    r=[pst_b, par_b[l]], w=[gtb])
                gates.append((gt, gtb))
            (rt, rtb), (it, itb) = gates
            at, atb = t32M.get()
            S.op("act", lambda e, rt=rt, at=at, c=c: e.activation(out=at[:], in_=rt[:], func=AF.Exp,
                                                                  scale=dd[:, c:c + 1]), r=[rtb, der_b[l]], w=[atb])
            S.op("act", lambda e, rt=rt, c=c: e.activation(out=rt[:], in_=rt[:], func=AF.Exp,
                                                           scale=dd[:, 3 + c:4 + c]), r=[rtb, der_b[l]], w=[rtb])
            S.op("act", lambda e, rt=rt: e.activation(out=rt[:], in_=rt[:], func=AF.Ln, bias=1.0, scale=-1.0),
                 r=[rtb], w=[rtb])
            S.op("act", lambda e, rt=rt: e.activation(out=rt[:], in_=rt[:], func=AF.Exp, scale=0.5),
                 r=[rtb], w=[rtb])
            yield
            S.op("dve", lambda e, rt=rt, it=it: e.tensor_tensor(out=it[:], in0=rt[:], in1=it[:], op=ALU.mult),
                 r=[rtb, itb], w=[itb])
            S.op("dve", lambda e, it=it, u=u: e.tensor_tensor(out=it[:], in0=it[:], in1=u[:], op=ALU.mult),
                 r=[itb, ub_], w=[itb])
            S.op("dve", lambda e, at=at, it=it, u=u, c=c: e.tensor_tensor_scan(
                out=u[:], data0=at[:], data1=it[:], initial=hst[:, c:c + 1], op0=ALU.mult, op1=ALU.add),
                r=[atb, itb, hst_b[c]], w=[ub_])
            S.op("dve", lambda e, u=u, c=c: e.tensor_copy(out=hst[:, c:c + 1], in_=u[:, TT - 1:TT]),
                 r=[ub_], w=[hst_b[c]])
            S.op("dve", lambda e, u=u, c=c: e.tensor_tensor(out=u[:], in0=u[:], in1=gg[:, c, :], op=ALU.mult),
                 r=[ub_, gg_b[c]], w=[ub_])
            yield
            tq, tqb = headnorm_sq(u, ub_)
            yield
            yield from headnorm_to_y(l, u, ub_, c, tq, tqb)
            yield

        eGs = []
        for c in range(3):
            lf, lfb = t32M.get()
            S.op("act", lambda e, c=c, lf=lf: e.activation(out=lf[:], in_=sg[:, c, :], func=AF.Ln,
                                                           bias=dd[:, 6 + c:7 + c], scale=dd[:, 9 + c:10 + c]),
                 r=[sg_b[c], der_b[l]], w=[lfb])
            kk, kkb = t32M.get()
            S.op("dve", lambda e, c=c, kk=kk: e.tensor_scalar(out=kk[:], in0=sg[:, c, :],
                                                               scalar1=dd[:, 12 + c:13 + c], scalar2=dd[:, 9 + c:10 + c],
                                                               op0=ALU.mult, op1=ALU.add),
                 r=[sg_b[c], der_b[l]], w=[kkb])
            yield
            G, Gb = t32M.get()
            S.op("dve", lambda e, lf=lf, G=G: e.tensor_tensor_scan(out=G[:], data0=smask, data1=lf[:], initial=0.0,
                                                                   op0=ALU.mult, op1=ALU.add),
                 r=[lfb, cst_b], w=[Gb])
            yield
            S.op("act", lambda e, lf=lf, G=G: e.activation(out=lf[:], in_=G[:], func=AF.Exp), r=[Gb], w=[lfb])
            S.op("act", lambda e, G=G: e.activation(out=G[:], in_=G[:], func=AF.Exp, scale=-1.0), r=[Gb], w=[Gb])
            yield
            eG, eGb = lf, lfb
            qg, qgb = qgp[:, c, :], qgp_b[c]
            S.op("dve", lambda e, c=c, qg=qg, eG=eG: e.scalar_tensor_tensor(
                out=qg, in0=sq[:, c, :], scalar=0.125, in1=eG[:], op0=ALU.mult, op1=ALU.mult),
                r=[sq_b[c], eGb], w=[qgb])
            S.op("dve", lambda e, c=c, eG=eG: e.tensor_copy(
                out=eGl[:, c, :], in_=eG[:].rearrange("p (c s) -> p c s", s=64)[:, :, 63]),
                r=[eGb], w=[eGl_b[c]])
            kg, kgb = t16M.get()
            S.op("dve", lambda e, kk=kk, G=G, kg=kg: e.tensor_tensor(out=kg[:, 0:TT], in0=kk[:], in1=G[:], op=ALU.mult),
                 r=[kkb, Gb], w=[kgb])
            kd, kdb = t16M.get()
            eG3 = eG[:].rearrange("p (c s) -> p c s", s=64)
            S.op("dve", lambda e, kg=kg, kd=kd, eG3=eG3: e.tensor_tensor(
                out=kd[:, 0:TT].rearrange("p (c s) -> p c s", s=64),
                in0=kg[:, 0:TT].rearrange("p (c s) -> p c s", s=64),
                in1=eG3[:, :, 63:64].to_broadcast([128, 8, 64]), op=ALU.mult),
                r=[kgb, eGb], w=[kdb])
            yield
            ptr, ptr_b = accM.get()
            for j in range(4):
                mm(ptr[:, 128 * j:128 * j + 128], kd[:, 128 * j:128 * j + 128], ident_b, True, True,
                   [kdb, cbf_b], [ptr_b], j == 3)
            S.op("act", lambda e, c=c, ptr=ptr: e.activation(out=kdT[:, c, :, :],
                                                             in_=ptr[:].rearrange("p (j f) -> p j f", j=4),
                                                             func=AF.Copy), r=[ptr_b], w=[kdT_b[c]])
            for h in range(2):
                hs = slice(64 * h, 64 * h + 64)
                psc, psc_b = accM.get()
                for j in range(4):
                    js = slice(128 * j, 128 * j + 128)
                    mm(psc[:, js], kg[hs, js], qg[hs, js], True, True, [kgb, qgb], [psc_b], j == 3)
                S.op("dve", lambda e, h=h, psc=psc: e.tensor_tensor(
                    out=Pm[:, h, :, :], in0=psc[:].rearrange("p (j t) -> p j t", j=4),
                    in1=mask_b.unsqueeze(1).to_broadcast([128, 4, 128]), op=ALU.mult),
                    r=[psc_b, cbf_b], w=[Pm_b[h]])
            yield
            for h in range(2):
                hs = slice(64 * h, 64 * h + 64)
                for j in range(4):
                    js = slice(128 * j, 128 * j + 128)
                    mm(po[c][hs, js], vtok[:, j, 128 * c + 64 * h:128 * c + 64 * h + 64], Pm[:, h, j, :],
                       j == 0, False, [vtok_b[j], Pm_b[h]], [po_b[c]], j == 3, skip=True)
            eGs.append((eGl[:, c, :], eGl_b[c], qg, qgb))
        for ch in range(8):
            j, hf = ch // 2, ch % 2
            rs_ = slice(64 * hf, 64 * hf + 64)
            cs = slice(64 * ch, 64 * ch + 64)
            pSb, pSb_b = accM.get()
            for c in range(3):
                for h in range(2):
                    hs = slice(64 * h, 64 * h + 64)
                    mm(pSb[hs, 64 * c:64 * c + 64], kdT[rs_, c, j, hs], vtok[rs_, j, 128 * c + 64 * h:128 * c + 64 * h + 64],
                       True, True, [kdT_b[c], vtok_b[j]], [pSb_b], (h == 1 and c == 2))
            for c in range(3):
                eG, eGb, qg, qgb = eGs[c]
                for h in range(2):
                    hs = slice(64 * h, 64 * h + 64)
                    mm(po[c][hs, cs], Sbf[hs, c, :], qg[hs, cs], False, True, [Sbf_b[c], qgb], [po_b[c]], h == 1, skip=True)
            for c in range(3):
                eG, eGb, qg, qgb = eGs[c]
                S.op("dve", lambda e, c=c, eG=eG, ch=ch, pSb=pSb: e.scalar_tensor_tensor(
                    out=Sst[:, c, :], in0=Sst[:, c, :], scalar=eG[:, ch:ch + 1], in1=pSb[:, 64 * c:64 * c + 64],
                    op0=ALU.mult, op1=ALU.add), r=[Sst_b[c], eGb, pSb_b], w=[Sst_b[c]])
                S.op("dve", lambda e, c=c: e.tensor_copy(out=Sbf[:, c, :], in_=Sst[:, c, :]),
                     r=[Sst_b[c]], w=[Sbf_b[c]])
            yield
        for c in range(3):
            o, ob = t32M.get()
            S.op("act", lambda e, o=o, c=c: e.activation(out=o[:], in_=po[c][:], func=AF.Copy), r=[po_b[c]], w=[ob])
            tq, tqb = t16M.get()
            S.op("act", lambda e, tq=tq, o=o: e.activation(out=tq[:, 0:TT], in_=o[:], func=AF.Square), r=[ob], w=[tqb])
            yield
            pst, pst_b = accM.get()
            mm(pst[:], ones64_b, tq[:, 0:TT], True, True, [tqb, cbf_b], [pst_b], True)
            rs, rsb = rstd_from_psum(pst, pst_b, t32M)
            yield
            S.op("dve", lambda e, o=o, rs=rs, c=c: e.scalar_tensor_tensor(
                out=o[:], in0=o[:], scalar=pc(l, "g_out", 3 + c), in1=rs[:], op0=ALU.mult, op1=ALU.mult),
                r=[ob, rsb, par_b[l]], w=[ob])
            S.op("dve", lambda e, o=o, c=c: e.tensor_tensor(out=hy[:, 3 + c, :], in0=o[:], in1=sgg[:, c, :], op=ALU.mult),
                 r=[ob, sgg_b[c]], w=[hy_b[3 + c]])
            yield

        for n in range(KC):
            if n % 2 == 0:
                wt, wtb = w_next(l, G_OUT(n // 2))
            wv = wt[:, 0:2048].rearrange("p (k n) -> p k n", k=8)
            nn = n % 2
            pst, pst_b = accM.get()
            for k in range(KC):
                mm(pst[:], wv[:, k, 128 * nn:128 * nn + 128], hy[:, k, :], k == 0, k == KC - 1,
                   [hy_b[k], wtb], [pst_b], k == KC - 1)
            xs = x_res[:, n, tsl]
            S.op("dve", lambda e, xs=xs, pst=pst: e.tensor_tensor(out=xs, in0=xs, in1=pst[:], op=ALU.add),
                 r=[pst_b, xb_[n][i]], w=[xb_[n][i]])
            if n % 2 == 1:
                yield

    def stage_F(l, i):
        tsl = slice(TT * i, TT * i + TT)
        hy, hy_b = hyF, hyF_b
        if i == 0:
            S.op("dve", lambda e: e.memset(fcar[:], 0.0), w=fcar_b)
        rmsnorm(l, i, "g_ffn", hy, hy_b, accF, t32F, t16F)
        yield
        for j in range(FC):
            wt, wtb = w_next(l, G_UP(j))
            conv = []
            for half in range(2):
                ci = half * FC + j
                wv = wt[:, 1024 * half:1024 * half + 1024].rearrange("p (k n) -> p k n", k=8)
                pst, pst_b = accF.get()
                for k in range(KC):
                    mm(pst[:], wv[:, k, :], hy[:, k, :], k == 0, k == KC - 1, [hy_b[k], wtb], [pst_b], k == KC - 1)
                pre, preb = t16F.get()
                S.op("act", lambda e, pre=pre, ci=ci: e.activation(out=pre[:, 0:2], in_=fcar[:, ci, :], func=AF.Copy),
                     r=[fcar_b[ci]], w=[preb])
                S.op("act", lambda e, pre=pre, pst=pst: e.activation(out=pre[:, 2:TT + 2], in_=pst[:], func=AF.Copy),
                     r=[pst_b], w=[preb])
                S.op("act", lambda e, pre=pre, ci=ci: e.activation(out=fcar[:, ci, :], in_=pre[:, TT:TT + 2], func=AF.Copy),
                     r=[preb], w=[fcar_b[ci]])
                cv, cvb = t16F.get()
                S.op("act", lambda e, pst=pst, cv=cv, ci=ci: e.activation(
                    out=cv[:, 0:TT], in_=pst[:], func=AF.Copy, scale=pc(l, "ffn_cw", 44 * 2 + ci)),
                    r=[pst_b, par_b[l]], w=[cvb])
                for tap in (1, 0):
                    S.op("dve", lambda e, pre=pre, cv=cv, ci=ci, tap=tap: e.scalar_tensor_tensor(
                        out=cv[:, 0:TT], in0=pre[:, tap:tap + TT], scalar=pc(l, "ffn_cw", 44 * tap + ci),
                        in1=cv[:, 0:TT], op0=ALU.mult, op1=ALU.add), r=[preb, cvb, par_b[l]], w=[cvb])
                conv.append((cv, cvb))
            (gc, gcb), (uc, ucb) = conv
            sgt, sgtb = t16F.get()
            S.op("act", lambda e, gc=gc, sgt=sgt: e.activation(out=sgt[:, 0:TT], in_=gc[:, 0:TT], func=AF.Silu),
                 r=[gcb], w=[sgtb])
            S.op("dve", lambda e, sgt=sgt, uc=uc, j=j: e.tensor_tensor(out=a_buf[:, j, :], in0=sgt[:, 0:TT],
                                                                        in1=uc[:, 0:TT], op=ALU.mult),
                 r=[sgtb, ucb], w=[a_b[j]])
            yield
        for n in range(KC):
            pst, pst_b = accF.get()
            for hf in range(2):
                wt, wtb = w_next(l, G_DN(n, hf))
                wv = wt[:, 0:1408].rearrange("p (k n) -> p k n", k=11)
                for k in range(11):
                    j = 11 * hf + k
                    mm(pst[:], wv[:, k, :], a_buf[:, j, :], j == 0, j == FC - 1, [a_b[j], wtb], [pst_b], j == FC - 1)
            xs = x_res[:, n, tsl]
            S.op("dve", lambda e, xs=xs, pst=pst: e.tensor_tensor(out=xs, in0=xs, in1=pst[:], op=ALU.add),
                 r=[pst_b, xb_[n][i]], w=[xb_[n][i]])
            yield
        s = i % 2
        S.dma(f"io{s}", lambda e: e.dma_start(
            out=io[s][:].rearrange("p (b j) -> p b j", b=4),
            in_=p_d[l][TT * i:TT * i + TT, :].rearrange("(b t) j -> t b j", t=128)), w=[io_b[s]])
        for jb in range(2):
            pst, pst_b = accF.get()
            for blk in range(4):
                S.op("pe", lambda e, pst=pst, blk=blk, jb=jb: e.transpose(
                    pst[:, 128 * blk:128 * blk + 128], io[s][:, 256 * blk + 128 * jb:256 * blk + 128 * jb + 128], ident_f),
                    r=[io_b[s], cst_b], w=[pst_b], signal=(blk == 3))
            S.op("act", lambda e, pst=pst, jb=jb: e.activation(out=pT[:, jb, :], in_=pst[:], func=AF.Copy),
                 r=[pst_b], w=[pT_b[jb]])
        for k in range(KC):
            xs = x_res[:, k, tsl]
            S.op("act", lambda e, k=k, xs=xs: e.activation(out=hy[:, k, :], in_=xs, func=AF.Copy),
                 r=[xb_[k][i]], w=[hy_b[k]])
        yield
        for n in range(KC):
            if n % 2 == 0:
                wt, wtb = w_next(l, G_PG(n // 2))
            wv = wt[:, 0:2048].rearrange("p (k n) -> p k n", k=8)
            nn = n % 2
            pst, pst_b = accF.get()
            for k in range(KC):
                mm(pst[:], wv[:, k, 128 * nn:128 * nn + 128], hy[:, k, :], k == 0, k == KC - 1,
                   [hy_b[k], wtb], [pst_b], k == KC - 1)
            S.op("act", lambda e, pst=pst, n=n: e.activation(out=a_buf[:, n, :], in_=pst[:], func=AF.Sigmoid),
                 r=[pst_b], w=[a_b[n]])
            if n % 2 == 1:
                yield
        wt, wtb = w_next(l, G_PP)
        wv = wt[:, 0:2048].rearrange("p (k n) -> p k n", k=2)
        for n in range(KC):
            pst, pst_b = accF.get()
            for k in range(2):
                mm(pst[:], wv[:, k, 128 * n:128 * n + 128], pT[:, k, :], k == 0, k == 1,
                   [pT_b[k], wtb], [pst_b], k == 1)
            tm, tmb = t32F.get()
            S.op("dve", lambda e, tm=tm, pst=pst, n=n: e.tensor_tensor(out=tm[:], in0=a_buf[:, n, :], in1=pst[:], op=ALU.mult),
                 r=[pst_b, a_b[n]], w=[tmb])
            xs = x_res[:, n, tsl]
            S.op("dve", lambda e, xs=xs, tm=tm: e.tensor_tensor(out=xs, in0=xs, in1=tm[:], op=ALU.add),
                 r=[tmb, xb_[n][i]], w=[xb_[n][i]])
        yield

    M_LEAD = 11

    def interleave(ga, gb):
        pa = pb = 0
        while ga is not None or gb is not None:
            run_a = (pa - M_LEAD) <= pb
            if gb is None:
                run_a = True
            if ga is None:
                run_a = False
            if run_a:
                try:
                    next(ga)
                    pa += 1
                except StopIteration:
                    ga = None
            else:
                try:
                    next(gb)
                    pb += 1
                except StopIteration:
                    gb = None

    def final_out(tiles_):
        for i in tiles_:
            if final_norm:
                pst, pst_b = accM.get()
                for k in range(KC):
                    tq, tqb = t16M.get()
                    xs = x_res[:, k, TT * i:TT * i + TT]
                    S.op("act", lambda e, tq=tq, xs=xs: e.activation(out=tq[:, 0:TT], in_=xs, func=AF.Square),
                         r=[xb_[k][i]], w=[tqb])
                    mm(pst[:], onesD[:], tq[:, 0:TT], k == 0, k == KC - 1, [tqb, cbf_b], [pst_b], True)
                rs, rsb = rstd_from_psum(pst, pst_b, t32M)
                for k in range(KC):
                    xs = x_res[:, k, TT * i:TT * i + TT]
                    S.op("dve", lambda e, k=k, xs=xs, rs=rs: e.scalar_tensor_tensor(
                        out=xs, in0=xs, scalar=pc(DEPTH - 1, "g_fin", k), in1=rs[:], op0=ALU.mult, op1=ALU.mult),
                        r=[xb_[k][i], rsb, par_b[DEPTH - 1]], w=[xb_[k][i]])
                yield
            for blk in range(4):
                tb = 4 * i + blk
                s = tb % 2
                for half in range(2):
                    pst, pst_b = accM.get()
                    for kk in range(4):
                        k = 4 * half + kk
                        S.op("pe", lambda e, pst=pst, kk=kk, k=k, tb=tb: e.transpose(
                            pst[:, 128 * kk:128 * kk + 128], x_res[:, k, 128 * tb:128 * tb + 128], ident_f),
                            r=[xb_[k][i], cst_b], w=[pst_b], signal=(kk == 3))
                    if half == 0:
                        S.op("act", lambda e, pst=pst, s=s: e.activation(out=io[s][:, 0:512], in_=pst[:], func=AF.Copy),
                             r=[pst_b], w=[io_b[s]])
                    else:
                        S.op("dve", lambda e, pst=pst, s=s: e.tensor_copy(out=io[s][:, 512:1024], in_=pst[:]),
                             r=[pst_b], w=[io_b[s]])
                S.dma(f"outd{s}", lambda e, s=s, tb=tb: e.dma_start(out=out_d[128 * tb:128 * tb + 128, :], in_=io[s][:]),
                      r=[io_b[s]])
                yield

    tiles = [(l, i) for l in range(n_layers) for i in range(NT)]
    gf_next = None
    for k in range(len(tiles) + 1):
        gm = stage_M(*tiles[k]) if k < len(tiles) else None
        gf = gf_next
        if k == 0:
            gf = load_x(range(4, T // 128))
        if k == len(tiles):
            FINAL_GEN = final_out(range(0, NT - 1))
            gm = FINAL_GEN
        interleave(gm, gf)
        if k < len(tiles):
            gf_next = stage_F(*tiles[k])
            next(gf_next)
    for _ in final_out([NT - 1]):
        pass

    nc.sync.wait_ge(S.sems["outd0"], S.cnt["outd0"])
    nc.sync.wait_ge(S.sems["outd1"], S.cnt["outd1"])
    es.close()
    return nc, rec


def build_program(**kw):
    _, order = _build(None, **kw)
    nc, _ = _build(order, **kw)
    return nc


def _chunks(v, n):
    return np.ascontiguousarray(np.asarray(v, np.float32).reshape(n, 128).T)


def _pack_params(inp, l):
    par = np.zeros((128, NPAR), np.float32)

    def put(name, arr):
        o, c = _PAR[name]
        assert arr.shape == (128, c), (name, arr.shape)
        par[:, o:o + c] = arr

    put("g_mix", _chunks(inp["norm_mix_gain"][l], 8))
    put("g_ffn", _chunks(inp["norm_ffn_gain"][l], 8))
    put("g_out", _chunks(inp["mix_out_gain"][l], 8))
    put("rg_cw", np.concatenate([_chunks(inp["rg_conv_w"][l][t], 3) for t in range(4)], axis=1))
    put("rg_cb", _chunks(inp["rg_conv_b"][l], 3))
    put("rg_br", _chunks(np.asarray(inp["rg_b_r"][l]).reshape(-1), 3))
    put("rg_bi", _chunks(np.asarray(inp["rg_b_i"][l]).reshape(-1), 3))
    put("rg_lam", _chunks(inp["rg_lambda"][l], 3))
    put("hg_lb0", _chunks(inp["hg_lower_bounds"][0], 3))
    put("hg_lb1", _chunks(inp["hg_lower_bounds"][min(1, DEPTH - 1)], 3))
    put("sc_cw", np.concatenate([_chunks(inp["sc_conv_w"][l][t], 2) for t in range(3)], axis=1))
    put("ffn_cw", np.concatenate([_chunks(inp["ffn_conv_w"][l][t], 44) for t in range(3)], axis=1))
    put("g_fin", _chunks(inp["final_norm_gain"], 8))
    return par


def _pack_gates(inp, l):
    out = np.zeros((128, 768), np.float32)
    for gi, name in enumerate(("rg_w_r", "rg_w_i")):
        w = np.asarray(inp[name][l], np.float32)
        for c in range(3):
            for h in range(2):
                out[64 * h:64 * h + 64, 384 * gi + 128 * c + 64 * h:384 * gi + 128 * c + 64 * h + 64] = w[2 * c + h]
    return out


def _consts():
    c = np.zeros((128, 896), np.float32)
    c[:, 0:128] = np.eye(128, dtype=np.float32)
    for h in range(2):
        c[64 * h:64 * h + 64, 128 + 64 * h:128 + 64 * h + 64] = 1.0 / 64.0
    s = np.arange(128)[:, None]
    t = np.arange(128)[None, :]
    c[:, 256:384] = ((s // 64 == t // 64) & (s <= t)).astype(np.float32)
    m = np.ones((128, 512), np.float32)
    m[:, ::64] = 0.0
    c[:, 384:896] = m
    return c


def make_in_maps(inp):
    inp = {k: np.asarray(v) for k, v in inp.items()}
    B = inp["x"].shape[0]
    shared = {
        "par": np.stack([_pack_params(inp, l) for l in range(DEPTH)]),
        "cst": _consts(),
        "w_in": np.ascontiguousarray(inp["w_in"], np.float32),
        "w_gates": np.stack([_pack_gates(inp, l) for l in range(DEPTH)]),
        "w_out": np.ascontiguousarray(inp["w_out"], np.float32),
        "w_up": np.ascontiguousarray(inp["ffn_w_up"], np.float32),
        "w_down": np.ascontiguousarray(inp["ffn_w_down"], np.float32),
        "w_pgate": np.ascontiguousarray(inp["ple_w_gate"], np.float32),
        "w_pproj": np.ascontiguousarray(inp["ple_w_proj"], np.float32),
    }
    maps = []
    for b in range(B):
        m = dict(shared)
        m["x"] = np.ascontiguousarray(inp["x"][b], np.float32)
        m["p"] = np.ascontiguousarray(inp["p"][:, b], np.float32)
        maps.append(m)
    return maps


def kernel(**inputs):
    nc = build_program()
    maps = make_in_maps(inputs)
    res = run_bass_kernel_spmd(nc, maps, core_ids=list(range(len(maps))))
    return np.stack([np.asarray(r["out"], np.float32) for r in res.results], axis=0)
```

```python
from contextlib import ExitStack
import numpy as np
import concourse.bass as bass
import concourse.mybir as mybir
from concourse.bass_utils import run_bass_kernel_spmd

F32 = mybir.dt.float32
BF16 = mybir.dt.bfloat16
AF = mybir.ActivationFunctionType
ALU = mybir.AluOpType

D = 1024
T = 2048
TT = 512
NT = T // TT
DEPTH = 2
DFF = 2816
NPROJ = 3072
PLE = 256
EPS = 1e-6
KC = D // 128
FC = DFF // 128
WSLOT = 2048

_PAR = {}
_off = 0
for _n, _c in [("g_mix", 8), ("g_ffn", 8), ("g_out", 8), ("rg_cw", 12), ("rg_cb", 3), ("rg_br", 3),
               ("rg_bi", 3), ("rg_lam", 3), ("hg_lb0", 3), ("hg_lb1", 3), ("sc_cw", 6),
               ("ffn_cw", 132), ("g_fin", 8)]:
    _PAR[_n] = (_off, _c)
    _off += _c
NPAR = _off


class Buf:
    __slots__ = ("name", "w", "r")

    def __init__(self, name):
        self.name = name
        self.w = None
        self.r = {}


class Sched:
    def __init__(self, nc, es):
        self.nc = nc
        self.es = es
        self.engs = {"pe": nc.tensor, "act": nc.scalar, "dve": nc.vector, "pool": nc.gpsimd, "sp": nc.sync}
        self.sems = {}
        self.cnt = {}
        self.seen = {e: {} for e in self.engs}
        self.snaps = {}
        for e in self.engs:
            self.newsem(e)

    def newsem(self, key):
        self.sems[key] = self.es.enter_context(self.nc.semaphore("s_" + str(key).replace(" ", "")))
        self.cnt[key] = 0

    def _waits(self, eng, r, w):
        need = {}

        def req(dep, raw):
            if dep is None:
                return
            key, c = dep
            if key == eng and eng == "pe":
                return
            if need.get(key, 0) < c:
                need[key] = c

        for b in r:
            req(b.w, True)
        for b in w:
            req(b.w, False)
            for key, c in b.r.items():
                req((key, c), False)
        E = self.engs[eng]
        seen = self.seen[eng]
        for key, c in need.items():
            if seen.get(key, 0) < c:
                E.wait_ge(self.sems[key], c)
                seen[key] = c
                snap = self.snaps.get((key, c))
                if snap:
                    for k2, v2 in snap.items():
                        if k2 != eng and seen.get(k2, 0) < v2:
                            seen[k2] = v2

    def _record(self, key, c, r, w):
        for b in r:
            if b.r.get(key, 0) < c:
                b.r[key] = c
        for b in w:
            b.w = (key, c)
            b.r = {}

    def op(self, eng, emit, r=(), w=(), signal=True):
        self._waits(eng, r, w)
        ins = emit(self.engs[eng])
        if signal:
            self.cnt[eng] += 1
            c = self.cnt[eng]
            ins.then_inc(self.sems[eng], 1)
            self.snaps[(eng, c)] = dict(self.seen[eng])
        else:
            c = self.cnt[eng] + 1
        self._record(eng, c, r, w)

    def dma(self, semkey, emit, r=(), w=(), q="sp"):
        self._waits(q, r, w)
        ins = emit(self.engs[q])
        self.cnt[semkey] += 16
        c = self.cnt[semkey]
        ins.then_inc(self.sems[semkey], 16)
        self.snaps[(semkey, c)] = dict(self.seen[q])
        self._record(semkey, c, r, w)


class Pool_:
    def __init__(self, nc, es, name, n, shape, dtype):
        self.tiles = [es.enter_context(nc.sbuf_tensor(f"sb_{name}{i}", shape, dtype)) for i in range(n)]
        self.bufs = [Buf(f"{name}{i}") for i in range(n)]
        self.i = 0

    def get(self):
        i = self.i
        self.i = (i + 1) % len(self.tiles)
        return self.tiles[i], self.bufs[i]


def G_IN(g): return g
def G_OUT(g): return 12 + g
def G_UP(j): return 16 + j
def G_DN(n, hf): return 38 + 2 * n + hf
def G_PG(g): return 54 + g
G_PP = 58
G_GATES = 59
NGRP = 60


def _build(order, n_layers=DEPTH, final_norm=True):
    record = order is None
    rec = []
    nc = bass.Bass("TRN2", target_bir_lowering=False)
    es = ExitStack()
    x_d = nc.dram_tensor("x", [T, D], F32, kind="ExternalInput").ap()
    p_d = nc.dram_tensor("p", [DEPTH, T, PLE], F32, kind="ExternalInput").ap()
    par_d = nc.dram_tensor("par", [DEPTH, 128, NPAR], F32, kind="ExternalInput").ap()
    cst_d = nc.dram_tensor("cst", [128, 128 * 3 + 512], F32, kind="ExternalInput").ap()
    w_in_d = nc.dram_tensor("w_in", [DEPTH, D, NPROJ], F32, kind="ExternalInput").ap()
    w_g_d = nc.dram_tensor("w_gates", [DEPTH, 128, 768], F32, kind="ExternalInput").ap()
    w_out_d = nc.dram_tensor("w_out", [DEPTH, D, D], F32, kind="ExternalInput").ap()
    w_up_d = nc.dram_tensor("w_up", [DEPTH, D, 2 * DFF], F32, kind="ExternalInput").ap()
    w_dn_d = nc.dram_tensor("w_down", [DEPTH, DFF, D], F32, kind="ExternalInput").ap()
    w_pg_d = nc.dram_tensor("w_pgate", [DEPTH, D, D], F32, kind="ExternalInput").ap()
    w_pp_d = nc.dram_tensor("w_pproj", [DEPTH, PLE, D], F32, kind="ExternalInput").ap()
    out_d = nc.dram_tensor("out", [T, D], F32, kind="ExternalOutput").ap()
    wscr = nc.dram_tensor("wscr", [n_layers, NGRP, 128, WSLOT], BF16, kind="Internal").ap()

    S = Sched(nc, es)

    def sb(name, shape, dt):
        return es.enter_context(nc.sbuf_tensor("sb_" + name, shape, dt))

    def ps(name, shape, dt=F32):
        return es.enter_context(nc.psum_tensor("ps_" + name, shape, dt))

    x_res = sb("x_res", [128, KC, T], F32)
    xb_ = [[Buf(f"x{k}_{i}") for i in range(NT)] for k in range(KC)]
    par = [sb(f"par{l}", [128, NPAR], F32) for l in range(DEPTH)]
    par_b = [Buf(f"par{l}") for l in range(DEPTH)]
    der = [sb(f"der{l}", [128, 16], F32) for l in range(DEPTH)]
    der_b = [Buf(f"der{l}") for l in range(DEPTH)]
    cst = sb("cst", [128, 128 * 3 + 512], F32)
    cst_b = Buf("cst")
    cbf = sb("cbf", [128, 128 * 3], BF16)
    cbf_b = Buf("cbf")
    onesD = sb("onesD", [128, 128], BF16)
    eps_t = sb("eps_t", [128, 1], F32)
    ident_f = cst[:, 0:128]
    smask = cst[:, 384:896]
    ident_b = cbf[:, 0:128]
    ones64_b = cbf[:, 128:256]
    mask_b = cbf[:, 256:384]

    CU = 1024
    NSTG, NCVT, NW = 2, 2, 4
    wst = [sb(f"wst{i}", [128, CU], F32) for i in range(NSTG)]
    wst_b = [Buf(f"wst{i}") for i in range(NSTG)]
    cvt = [sb(f"cvt{i}", [128, CU], BF16) for i in range(NCVT)]
    cvt_b = [Buf(f"cvt{i}") for i in range(NCVT)]
    wbf = [sb(f"wbf{i}", [128, WSLOT], BF16) for i in range(NW)]
    wbf_b = [Buf(f"wbf{i}") for i in range(NW)]
    wgt = sb("wgt", [128, 768], BF16)
    wgt_b = Buf("wgt")
    for i in range(NSTG):
        S.newsem(("cvi", i))
    for i in range(NCVT):
        S.newsem(("cvo", i))
    for i in range(NW):
        S.newsem(("ws", i))
    for k in ("io0", "io1", "misc", "outd0", "outd1", "wg"):
        S.newsem(k)

    hyM = sb("hyM", [128, KC, TT], BF16)
    hyM_b = [Buf(f"hyM{k}") for k in range(KC)]
    hyF = sb("hyF", [128, KC, TT], BF16)
    hyF_b = [Buf(f"hyF{k}") for k in range(KC)]
    au = sb("au", [128, 3, TT + 3], BF16)
    au_b = [Buf(f"au{c}") for c in range(3)]
    gg = sb("gg", [128, 3, TT], BF16)
    gg_b = [Buf(f"gg{c}") for c in range(3)]
    sq = sb("sq", [128, 3, TT], BF16)
    sq_b = [Buf(f"sq{c}") for c in range(3)]
    sg = sb("sg", [128, 3, TT], F32)
    sg_b = [Buf(f"sg{c}") for c in range(3)]
    vtok = sb("vtok", [128, 4, 384], BF16)
    vtok_b = [Buf(f"vtok{j}") for j in range(4)]
    sgg = sb("sgg", [128, 3, TT], BF16)
    sgg_b = [Buf(f"sgg{c}") for c in range(3)]
    cb = sb("cb", [128, 2, TT], BF16)
    cb_b = [Buf(f"cb{c}") for c in range(2)]
    ccv = sb("ccv", [128, 2, TT + 2], BF16)
    ccv_b = [Buf(f"ccv{c}") for c in range(2)]
    qgp = sb("qgp", [128, 3, TT], BF16)
    qgp_b = [Buf(f"qgp{c}") for c in range(3)]
    eGl = sb("eGl", [128, 3, 8], F32)
    eGl_b = [Buf(f"eGl{c}") for c in range(3)]
    a_buf = sb("a_buf", [128, FC, TT], BF16)
    a_b = [Buf(f"a{j}") for j in range(FC)]
    fcar = sb("fcar", [128, 2 * FC, 2], BF16)
    fcar_b = [Buf(f"fcar{j}") for j in range(2 * FC)]
    io = [sb(f"io{i}", [128, 1024], F32) for i in range(2)]
    io_b = [Buf(f"io{i}") for i in range(2)]
    pT = sb("pT", [128, 2, TT], BF16)
    pT_b = [Buf("pT0"), Buf("pT1")]
    hst = sb("hst", [128, 3], F32)
    hst_b = [Buf(f"hst{c}") for c in range(3)]
    Sst = sb("Sst", [128, 3, 64], F32)
    Sbf = sb("Sbf", [128, 3, 64], BF16)
    Sst_b = [Buf(f"Sst{c}") for c in range(3)]
    Sbf_b = [Buf(f"Sbf{c}") for c in range(3)]
    kdT = sb("kdT", [128, 3, 4, 128], BF16)
    kdT_b = [Buf(f"kdT{c}") for c in range(3)]
    Pm = sb("Pm", [128, 2, 4, 128], BF16)
    Pm_b = [Buf("Pm0"), Buf("Pm1")]
    t32M = Pool_(nc, es, "t32M_", 5, [128, TT], F32)
    t16M = Pool_(nc, es, "t16M_", 5, [128, TT + 2], BF16)
    t32F = Pool_(nc, es, "t32F_", 2, [128, TT], F32)
    t16F = Pool_(nc, es, "t16F_", 6, [128, TT + 2], BF16)

    class Banks:
        def __init__(self, name, n):
            self.t = [ps(f"{name}{i}", [128, 512]) for i in range(n)]
            self.b = [Buf(f"{name}{i}") for i in range(n)]
            self.i = 0

        def get(self):
            i = self.i
            self.i = (i + 1) % len(self.t)
            return self.t[i], self.b[i]

    accM = Banks("accM", 3)
    accF = Banks("accF", 2)
    po = [ps(f"po{i}", [128, 512]) for i in range(3)]
    po_b = [Buf(f"po{i}") for i in range(3)]

    def pc(l, name, j=0, n=1):
        o, _ = _PAR[name]
        return par[l][:, o + j:o + j + n]

    def mm(out, lhsT, rhs, start, stop, r, w, signal, skip=False):
        S.op("pe", lambda e: e.matmul(out, lhsT, rhs, start=start, stop=stop, skip_group_check=skip),
             r=r, w=w, signal=signal)

    S.dma("misc", lambda e: e.dma_start(out=cst[:], in_=cst_d), w=[cst_b])
    for l in range(DEPTH):
        S.dma("misc", lambda e, l=l: e.dma_start(out=par[l][:], in_=par_d[l]), w=[par_b[l]])
    for b_ in [cst_b] + par_b:
        b_.w = ("misc", S.cnt["misc"])
    S.op("dve", lambda e: e.tensor_copy(out=cbf[:], in_=cst[:, 0:384]), r=[cst_b], w=[cbf_b])
    S.op("dve", lambda e: e.memset(onesD[:], 1.0 / D), w=[cbf_b])
    S.op("dve", lambda e: e.memset(eps_t[:], EPS), w=[cbf_b])
    for l in range(DEPTH):
        dd = der[l]
        lam = pc(l, "rg_lam", 0, 3)
        S.op("act", lambda e, dd=dd, lam=lam: e.activation(out=dd[:, 0:3], in_=lam, func=AF.Exp, scale=-1.0),
             r=[par_b[l]], w=[der_b[l]])
        S.op("act", lambda e, dd=dd: e.activation(out=dd[:, 0:3], in_=dd[:, 0:3], func=AF.Ln, bias=1.0),
             r=[der_b[l]], w=[der_b[l]])
        S.op("dve", lambda e, dd=dd: e.tensor_scalar(out=dd[:, 3:6], in0=dd[:, 0:3], scalar1=-16.0, scalar2=None,
                                                     op0=ALU.mult), r=[der_b[l]], w=[der_b[l]])
        S.op("dve", lambda e, dd=dd: e.tensor_scalar(out=dd[:, 0:3], in0=dd[:, 0:3], scalar1=-8.0, scalar2=None,
                                                     op0=ALU.mult), r=[der_b[l]], w=[der_b[l]])
        if l == 0:
            S.op("dve", lambda e, dd=dd: e.memset(dd[:, 6:9], 0.0), w=[der_b[l]])
        else:
            S.op("dve", lambda e, dd=dd, l=l: e.tensor_tensor(out=dd[:, 6:9], in0=pc(l, "hg_lb1", 0, 3),
                                                              in1=pc(l, "hg_lb0", 0, 3), op=ALU.subtract),
                 r=[par_b[l]], w=[der_b[l]])
            S.op("act", lambda e, dd=dd: e.activation(out=dd[:, 6:9], in_=dd[:, 6:9], func=AF.Sigmoid),
                 r=[der_b[l]], w=[der_b[l]])
        S.op("dve", lambda e, dd=dd: e.tensor_scalar(out=dd[:, 9:12], in0=dd[:, 6:9], scalar1=-1.0, scalar2=1.0,
                                                     op0=ALU.mult, op1=ALU.add), r=[der_b[l]], w=[der_b[l]])
        S.op("dve", lambda e, dd=dd: e.tensor_scalar(out=dd[:, 12:15], in0=dd[:, 9:12], scalar1=-1.0, scalar2=None,
                                                     op0=ALU.mult), r=[der_b[l]], w=[der_b[l]])

    def group_units(l, g):
        def kp(ap_, k):
            return ap_.rearrange("(k p) n -> p k n", p=128), k
        if g < 12:
            c0 = 256 * g
            return [(w_in_d[l][512 * h:512 * h + 512, c0:c0 + 256], 4, 1024 * h, 1024) for h in range(2)]
        if g < 16:
            c0 = 256 * (g - 12)
            return [(w_out_d[l][512 * h:512 * h + 512, c0:c0 + 256], 4, 1024 * h, 1024) for h in range(2)]
        if g < 38:
            j = g - 16
            return [(w_up_d[l][:, h * DFF + 128 * j:h * DFF + 128 * j + 128], 8, 1024 * h, 1024) for h in range(2)]
        if g < 54:
            n, hf = (g - 38) // 2, (g - 38) % 2
            r0 = 1408 * hf
            return [(w_dn_d[l][r0:r0 + 1024, 128 * n:128 * n + 128], 8, 0, 1024),
                    (w_dn_d[l][r0 + 1024:r0 + 1408, 128 * n:128 * n + 128], 3, 1024, 384)]
        if g < 58:
            c0 = 256 * (g - 54)
            return [(w_pg_d[l][512 * h:512 * h + 512, c0:c0 + 256], 4, 1024 * h, 1024) for h in range(2)]
        if g == G_PP:
            return [(w_pp_d[l][128 * h:128 * h + 128, :], None, 1024 * h, 1024) for h in range(2)]
        return [(w_g_d[l], None, 0, 768)]

    GN = [2048] * 38 + [1408] * 16 + [2048] * 5 + [768]
    scr_b = [[[Buf(f"scr{l}_{g}_{u}") for u in range(2)] for g in range(NGRP)] for l in range(n_layers)]
    conv_order = list(range(12)) + [G_GATES] + list(range(12, 59))
    cvs = {"n": [0] * n_layers, "tot": 0}
    S.newsem(("cvj", 0))
    S.newsem(("cvj", 1))
    stg_slots = [(wst[0], wst_b[0], ("cvi", 0)), (wst[1], wst_b[1], ("cvi", 1))]
    stg_slots0 = [(wst[0], wst_b[0], ("cvi", 0)), (wst[1], wst_b[1], ("cvi", 1)),
                  (io[0], io_b[0], "io0"), (io[1], io_b[1], "io1")]
    cvt_slots = [(cvt[0][:, :], [cvt_b[0]], ("cvo", 0)), (cvt[1][:, :], [cvt_b[1]], ("cvo", 1))]
    cvt_slots0 = list(cvt_slots)
    for m in range(6, 11):
        S.newsem(("cvo", m))
        cvt_slots0.append((a_buf[:, 2 * m:2 * m + 2, :].rearrange("p a t -> p (a t)"), [a_b[2 * m], a_b[2 * m + 1]], ("cvo", m)))

    def conv_emit(l):
        idx = cvs["n"][l]
        if idx >= NGRP:
            return False
        cvs["n"][l] += 1
        g = conv_order[idx]
        units = group_units(l, g)
        for ui, (src, k, off, n) in enumerate(units):
            t = cvs["tot"]
            cvs["tot"] += 1
            sl_ = stg_slots0 if l == 0 else stg_slots
            st_t, st_b, st_sem = sl_[t % len(sl_)]
            cl_ = cvt_slots0 if (l == 0 and idx < 17) else cvt_slots
            cv_t, cv_b, cv_sem = cl_[t % len(cl_)]
            inq = "sp"
            if k is None:
                S.dma(st_sem, lambda e, st_t=st_t, src=src, n=n: e.dma_start(out=st_t[:, 0:n], in_=src),
                      w=[st_b], q=inq)
            else:
                S.dma(st_sem, lambda e, st_t=st_t, src=src, n=n, k=k: e.dma_start(
                    out=st_t[:, 0:n].rearrange("p (k n) -> p k n", k=k),
                    in_=src.rearrange("(k p) n -> p k n", p=128)), w=[st_b], q=inq)
            ce = ("act", "dve", "pool", "act", "dve")[t % 5] if l == 0 else ("pool", "pool", "act")[t % 3]
            if ce == "act":
                S.op("act", lambda e, st_t=st_t, cv_t=cv_t, n=n: e.activation(out=cv_t[:, 0:n], in_=st_t[:, 0:n], func=AF.Copy),
                     r=[st_b], w=cv_b)
            else:
                S.op(ce, lambda e, st_t=st_t, cv_t=cv_t, n=n: e.tensor_copy(out=cv_t[:, 0:n], in_=st_t[:, 0:n]),
                     r=[st_b], w=cv_b)
            S.dma(cv_sem, lambda e, cv_t=cv_t, off=off, n=n, g=g: e.dma_start(out=wscr[l, g, :, off:off + n],
                                                                            in_=cv_t[:, 0:n]),
                  r=cv_b, w=[scr_b[l][g][ui]], q="pool")
        return True

    def conv_upto(l, g):
        pos = conv_order.index(g)
        while cvs["n"][l] <= pos:
            conv_emit(l)

    ws = {"issued": 0, "next": 0}
    stream = order if not record else None

    def issue(l, g, qi):
        conv_upto(l, min(g, 58) if l > 0 else g)
        if l == 0 and g < 58:
            conv_upto(0, min(g + 8, 58))
        if l > 0:
            while conv_emit(l):
                pass
        sl = qi % NW
        n = GN[g]
        S.dma(("ws", sl), lambda e: e.dma_start(out=wbf[sl][:, 0:n], in_=wscr[l, g, :, 0:n]),
              r=scr_b[l][g], w=[wbf_b[sl]])

    def w_next(l, g):
        qi = ws["next"]
        ws["next"] += 1
        if record:
            rec.append((l, g))
            issue(l, g, qi)
        else:
            assert stream[qi] == (l, g), (qi, stream[qi], (l, g))
            while ws["issued"] <= min(qi + NW - 1, len(stream) - 1):
                q2 = ws["issued"]
                l2, g2 = stream[q2]
                issue(l2, g2, q2)
                if l2 == 0 and n_layers > 1 and cvs["n"][0] >= NGRP and q2 >= 59 and q2 % 2 == 0:
                    conv_emit(1)
                ws["issued"] += 1
        sl = qi % NW
        return wbf[sl], wbf_b[sl]

    def load_gates(l):
        conv_upto(l, G_GATES)
        S.dma("wg", lambda e: e.dma_start(out=wgt[:], in_=wscr[l, G_GATES, :, 0:768]), r=scr_b[l][G_GATES], w=[wgt_b])

    def rstd_from_psum(pst, pst_b, t32):
        tl, tlb = t32.get()
        S.op("act", lambda e: e.activation(out=tl[:], in_=pst[:], func=AF.Ln, bias=eps_t[:, 0:1], scale=1.0),
             r=[pst_b, cbf_b], w=[tlb])
        S.op("act", lambda e: e.activation(out=tl[:], in_=tl[:], func=AF.Exp, scale=-0.5), r=[tlb], w=[tlb])
        return tl, tlb

    def rmsnorm(l, i, gname, hy, hy_b, acc, t32, t16):
        pst, pst_b = acc.get()
        for k in range(KC):
            tq, tqb = t16.get()
            xs = x_res[:, k, TT * i:TT * i + TT]
            S.op("act", lambda e, tq=tq, xs=xs: e.activation(out=tq[:, 0:TT], in_=xs, func=AF.Square),
                 r=[xb_[k][i]], w=[tqb])
            mm(pst[:], onesD[:], tq[:, 0:TT], k == 0, k == KC - 1, [tqb, cbf_b], [pst_b], True)
        rs, rsb = rstd_from_psum(pst, pst_b, t32)
        for k in range(KC):
            xs = x_res[:, k, TT * i:TT * i + TT]
            S.op("dve", lambda e, k=k, xs=xs: e.scalar_tensor_tensor(out=hy[:, k, :], in0=xs, scalar=pc(l, gname, k),
                                                                     in1=rs[:], op0=ALU.mult, op1=ALU.mult),
                 r=[xb_[k][i], rsb, par_b[l]], w=[hy_b[k]])

    def headnorm_sq(src, srcb):
        tq, tqb = t16M.get()
        S.op("act", lambda e: e.activation(out=tq[:, 0:TT], in_=src[:], func=AF.Square), r=[srcb], w=[tqb])
        return tq, tqb

    def headnorm_to_y(l, src, srcb, ychunk, tq, tqb):
        pst, pst_b = accM.get()
        mm(pst[:], ones64_b, tq[:, 0:TT], True, True, [tqb, cbf_b], [pst_b], True)
        rs, rsb = rstd_from_psum(pst, pst_b, t32M)
        yield
        S.op("dve", lambda e: e.scalar_tensor_tensor(out=hyM[:, ychunk, :], in0=src[:], scalar=pc(l, "g_out", ychunk),
                                                     in1=rs[:], op0=ALU.mult, op1=ALU.mult),
             r=[srcb, rsb, par_b[l]], w=[hyM_b[ychunk]])

    def load_x(tbs):
        for tb in tbs:
            s = tb % 2
            S.dma(f"io{s}", lambda e, s=s, tb=tb: e.dma_start(out=io[s][:], in_=x_d[128 * tb:128 * tb + 128, :]),
                  w=[io_b[s]])
            i = tb // 4
            for half in range(2):
                pst, pst_b = accM.get()
                for kk in range(4):
                    k = 4 * half + kk
                    S.op("pe", lambda e, pst=pst, kk=kk, k=k, s=s: e.transpose(pst[:, 128 * kk:128 * kk + 128],
                                                                             io[s][:, 128 * k:128 * k + 128], ident_f),
                         r=[io_b[s], cst_b], w=[pst_b], signal=(kk == 3))
                dst = x_res[:, 4 * half:4 * half + 4, 128 * tb:128 * tb + 128]
                S.op("act" if half == 0 else "dve",
                     (lambda e, dst=dst, pst=pst: e.activation(out=dst, in_=pst[:].rearrange("p (k t) -> p k t", k=4),
                                                               func=AF.Copy)) if half == 0 else
                     (lambda e, dst=dst, pst=pst: e.tensor_copy(out=dst, in_=pst[:].rearrange("p (k t) -> p k t", k=4))),
                     r=[pst_b], w=[xb_[4 * half + kk][i] for kk in range(4)])
            yield

    for _ in load_x(range(0, 4)):
        pass

    def stage_M(l, i):
        dd = der[l]
        tsl = slice(TT * i, TT * i + TT)
        hy, hy_b = hyM, hyM_b
        if i == 0:
            load_gates(l)
            for c in range(3):
                S.op("dve", lambda e, c=c: e.memset(au[:, c, 0:3], 0.0), w=[au_b[c]])
                S.op("dve", lambda e, c=c: e.memset(hst[:, c:c + 1], 0.0), w=[hst_b[c]])
                S.op("dve", lambda e, c=c: e.memset(Sst[:, c, :], 0.0), w=[Sst_b[c]])
                S.op("dve", lambda e, c=c: e.memset(Sbf[:, c, :], 0.0), w=[Sbf_b[c]])
            for c in range(2):
                S.op("dve", lambda e, c=c: e.memset(ccv[:, c, 0:2], 0.0), w=[ccv_b[c]])
        rmsnorm(l, i, "g_mix", hy, hy_b, accM, t32M, t16M)
        yield
        cc_tmp = {}
        wt, wtb = None, None
        for n in range(24):
            if n % 2 == 0:
                wt, wtb = w_next(l, G_IN(n // 2))
            wv = wt[:, 0:2048].rearrange("p (k n) -> p k n", k=8)
            nn = n % 2
            if 12 <= n < 15:
                c = n - 12
                pst, pst_b = accM.get()
                for jb in range(4):
                    for k in range(KC):
                        mm(pst[:, 128 * jb:128 * jb + 128], hy[:, k, 128 * jb:128 * jb + 128],
                           wv[:, k, 128 * nn:128 * nn + 128], k == 0, k == KC - 1,
                           [hy_b[k], wtb], [pst_b], (k == KC - 1 and jb == 3))
                S.op("act", lambda e, pst=pst, c=c: e.activation(
                    out=vtok[:, :, 128 * c:128 * c + 128], in_=pst[:].rearrange("p (j v) -> p j v", j=4),
                    func=AF.Copy), r=[pst_b], w=vtok_b)
            else:
                pst, pst_b = accM.get()
                for k in range(KC):
                    mm(pst[:], wv[:, k, 128 * nn:128 * nn + 128], hy[:, k, :], k == 0, k == KC - 1,
                       [hy_b[k], wtb], [pst_b], k == KC - 1)
                if n < 3:
                    S.op("act", lambda e, pst=pst, n=n: e.activation(out=au[:, n, 3:TT + 3], in_=pst[:], func=AF.Copy),
                         r=[pst_b], w=[au_b[n]])
                elif n < 6:
                    S.op("act", lambda e, pst=pst, n=n: e.activation(out=gg[:, n - 3, :], in_=pst[:],
                                                                     func=AF.Gelu_apprx_tanh), r=[pst_b], w=[gg_b[n - 3]])
                elif n < 9:
                    S.op("act", lambda e, pst=pst, n=n: e.activation(out=sq[:, n - 6, :], in_=pst[:], func=AF.Silu),
                         r=[pst_b], w=[sq_b[n - 6]])
                elif n < 12:
                    S.op("act", lambda e, pst=pst, n=n: e.activation(out=sg[:, n - 9, :], in_=pst[:], func=AF.Sigmoid),
                         r=[pst_b], w=[sg_b[n - 9]])
                elif n < 18:
                    S.op("act", lambda e, pst=pst, n=n: e.activation(out=sgg[:, n - 15, :], in_=pst[:], func=AF.Silu),
                         r=[pst_b], w=[sgg_b[n - 15]])
                elif n < 20:
                    S.op("act", lambda e, pst=pst, n=n: e.activation(out=cb[:, n - 18, :], in_=pst[:], func=AF.Copy),
                         r=[pst_b], w=[cb_b[n - 18]])
                elif n < 22:
                    cct, cctb = t32M.get()
                    cc_tmp[n - 20] = (cct, cctb)
                    S.op("act", lambda e, pst=pst, cct=cct: e.activation(out=cct[:], in_=pst[:], func=AF.Copy),
                         r=[pst_b], w=[cctb])
                else:
                    c = n - 22
                    cct, cctb = cc_tmp[c]
                    S.op("dve", lambda e, pst=pst, c=c, cct=cct: e.tensor_tensor(out=ccv[:, c, 2:TT + 2], in0=cct[:],
                                                                                 in1=pst[:], op=ALU.mult),
                         r=[pst_b, cctb], w=[ccv_b[c]])
            if n % 2 == 1 and n < 20:
                yield

        for c in range(2):
            u, ub_ = t32M.get()
            S.op("dve", lambda e, c=c, u=u: e.tensor_scalar(out=u[:], in0=ccv[:, c, 2:TT + 2],
                                                             scalar1=pc(l, "sc_cw", 2 * 2 + c), scalar2=None,
                                                             op0=ALU.mult), r=[ccv_b[c], par_b[l]], w=[ub_])
            for tap in (1, 0):
                S.op("dve", lambda e, c=c, u=u, tap=tap: e.scalar_tensor_tensor(
                    out=u[:], in0=ccv[:, c, tap:tap + TT], scalar=pc(l, "sc_cw", 2 * tap + c), in1=u[:],
                    op0=ALU.mult, op1=ALU.add), r=[ccv_b[c], ub_, par_b[l]], w=[ub_])
            S.op("act", lambda e, c=c: e.activation(out=ccv[:, c, 0:2], in_=ccv[:, c, TT:TT + 2], func=AF.Copy),
                 r=[ccv_b[c]], w=[ccv_b[c]])
            S.op("dve", lambda e, u=u, c=c: e.tensor_tensor(out=u[:], in0=u[:], in1=cb[:, c, :], op=ALU.mult),
                 r=[ub_, cb_b[c]], w=[ub_])
            yield
            tq, tqb = headnorm_sq(u, ub_)
            yield
            yield from headnorm_to_y(l, u, ub_, 6 + c, tq, tqb)
            yield

        for c in range(3):
            u, ub_ = t32M.get()
            S.op("dve", lambda e, c=c, u=u: e.tensor_scalar(out=u[:], in0=au[:, c, 3:TT + 3],
                                                             scalar1=pc(l, "rg_cw", 3 * 3 + c), scalar2=pc(l, "rg_cb", c),
                                                             op0=ALU.mult, op1=ALU.add),
                 r=[au_b[c], par_b[l]], w=[ub_])
            for tap in (2, 1, 0):
                S.op("dve", lambda e, c=c, u=u, tap=tap: e.scalar_tensor_tensor(
                    out=u[:], in0=au[:, c, tap:tap + TT], scalar=pc(l, "rg_cw", 3 * tap + c), in1=u[:],
                    op0=ALU.mult, op1=ALU.add), r=[au_b[c], ub_, par_b[l]], w=[ub_])
            yield
            S.op("act", lambda e, c=c: e.activation(out=au[:, c, 0:3], in_=au[:, c, TT:TT + 3], func=AF.Copy),
                 r=[au_b[c]], w=[au_b[c]])
            u16, u16b = t16M.get()
            S.op("act", lambda e, u=u, u16=u16: e.activation(out=u16[:, 0:TT], in_=u[:], func=AF.Copy),
                 r=[ub_], w=[u16b])
            yield
            gates = []
            for gi, bname in ((0, "rg_br"), (1, "rg_bi")):
                pst, pst_b = accM.get()
                mm(pst[:], wgt[:, 384 * gi + 128 * c:384 * gi + 128 * c + 128], u16[:, 0:TT], True, True,
                   [u16b, wgt_b], [pst_b], True)
                gt, gtb = t32M.get()
                S.op("act", lambda e, pst=pst, gt=gt, bname=bname, c=c: e.activation(
                    out=gt[:], in_=pst[:], func=AF.Sigmoid, bias=pc(l, bname, c)),
                    r=[pst_b, par_b[l]], w=[gtb])
                gates.append((gt, gtb))
            (rt, rtb), (it, itb) = gates
            at, atb = t32M.get()
            S.op("act", lambda e, rt=rt, at=at, c=c: e.activation(out=at[:], in_=rt[:], func=AF.Exp,
                                                                  scale=dd[:, c:c + 1]), r=[rtb, der_b[l]], w=[atb])
            S.op("act", lambda e, rt=rt, c=c: e.activation(out=rt[:], in_=rt[:], func=AF.Exp,
                                                           scale=dd[:, 3 + c:4 + c]), r=[rtb, der_b[l]], w=[rtb])
            S.op("act", lambda e, rt=rt: e.activation(out=rt[:], in_=rt[:], func=AF.Ln, bias=1.0, scale=-1.0),
                 r=[rtb], w=[rtb])
            S.op("act", lambda e, rt=rt: e.activation(out=rt[:], in_=rt[:], func=AF.Exp, scale=0.5),
                 r=[rtb], w=[rtb])
            yield
            S.op("dve", lambda e, rt=rt, it=it: e.tensor_tensor(out=it[:], in0=rt[:], in1=it[:], op=ALU.mult),
                 r=[rtb, itb], w=[itb])
            S.op("dve", lambda e, it=it, u=u: e.tensor_tensor(out=it[:], in0=it[:], in1=u[:], op=ALU.mult),
                 r=[itb, ub_], w=[itb])
            S.op("dve", lambda e, at=at, it=it, u=u, c=c: e.tensor_tensor_scan(
                out=u[:], data0=at[:], data1=it[:], initial=hst[:, c:c + 1], op0=ALU.mult, op1=ALU.add),
                r=[atb, itb, hst_b[c]], w=[ub_])
            S.op("dve", lambda e, u=u, c=c: e.tensor_copy(out=hst[:, c:c + 1], in_=u[:, TT - 1:TT]),
                 r=[ub_], w=[hst_b[c]])
            S.op("dve", lambda e, u=u, c=c: e.tensor_tensor(out=u[:], in0=u[:], in1=gg[:, c, :], op=ALU.mult),
                 r=[ub_, gg_b[c]], w=[ub_])
            yield
            tq, tqb = headnorm_sq(u, ub_)
            yield
            yield from headnorm_to_y(l, u, ub_, c, tq, tqb)
            yield

        eGs = []
        for c in range(3):
            lf, lfb = t32M.get()
            S.op("act", lambda e, c=c, lf=lf: e.activation(out=lf[:], in_=sg[:, c, :], func=AF.Ln,
                                                           bias=dd[:, 6 + c:7 + c], scale=dd[:, 9 + c:10 + c]),
                 r=[sg_b[c], der_b[l]], w=[lfb])
            kk, kkb = t32M.get()
            S.op("dve", lambda e, c=c, kk=kk: e.tensor_scalar(out=kk[:], in0=sg[:, c, :],
                                                               scalar1=dd[:, 12 + c:13 + c], scalar2=dd[:, 9 + c:10 + c],
                                                               op0=ALU.mult, op1=ALU.add),
                 r=[sg_b[c], der_b[l]], w=[kkb])
            yield
            G, Gb = t32M.get()
            S.op("dve", lambda e, lf=lf, G=G: e.tensor_tensor_scan(out=G[:], data0=smask, data1=lf[:], initial=0.0,
                                                                   op0=ALU.mult, op1=ALU.add),
                 r=[lfb, cst_b], w=[Gb])
            yield
            S.op("act", lambda e, lf=lf, G=G: e.activation(out=lf[:], in_=G[:], func=AF.Exp), r=[Gb], w=[lfb])
            S.op("act", lambda e, G=G: e.activation(out=G[:], in_=G[:], func=AF.Exp, scale=-1.0), r=[Gb], w=[Gb])
            yield
            eG, eGb = lf, lfb
            qg, qgb = qgp[:, c, :], qgp_b[c]
            S.op("dve", lambda e, c=c, qg=qg, eG=eG: e.scalar_tensor_tensor(
                out=qg, in0=sq[:, c, :], scalar=0.125, in1=eG[:], op0=ALU.mult, op1=ALU.mult),
                r=[sq_b[c], eGb], w=[qgb])
            S.op("dve", lambda e, c=c, eG=eG: e.tensor_copy(
                out=eGl[:, c, :], in_=eG[:].rearrange("p (c s) -> p c s", s=64)[:, :, 63]),
                r=[eGb], w=[eGl_b[c]])
            kg, kgb = t16M.get()
            S.op("dve", lambda e, kk=kk, G=G, kg=kg: e.tensor_tensor(out=kg[:, 0:TT], in0=kk[:], in1=G[:], op=ALU.mult),
                 r=[kkb, Gb], w=[kgb])
            kd, kdb = t16M.get()
            eG3 = eG[:].rearrange("p (c s) -> p c s", s=64)
            S.op("dve", lambda e, kg=kg, kd=kd, eG3=eG3: e.tensor_tensor(
                out=kd[:, 0:TT].rearrange("p (c s) -> p c s", s=64),
                in0=kg[:, 0:TT].rearrange("p (c s) -> p c s", s=64),
                in1=eG3[:, :, 63:64].to_broadcast([128, 8, 64]), op=ALU.mult),
                r=[kgb, eGb], w=[kdb])
            yield
            ptr, ptr_b = accM.get()
            for j in range(4):
                mm(ptr[:, 128 * j:128 * j + 128], kd[:, 128 * j:128 * j + 128], ident_b, True, True,
                   [kdb, cbf_b], [ptr_b], j == 3)
            S.op("act", lambda e, c=c, ptr=ptr: e.activation(out=kdT[:, c, :, :],
                                                             in_=ptr[:].rearrange("p (j f) -> p j f", j=4),
                                                             func=AF.Copy), r=[ptr_b], w=[kdT_b[c]])
            for h in range(2):
                hs = slice(64 * h, 64 * h + 64)
                psc, psc_b = accM.get()
                for j in range(4):
                    js = slice(128 * j, 128 * j + 128)
                    mm(psc[:, js], kg[hs, js], qg[hs, js], True, True, [kgb, qgb], [psc_b], j == 3)
                S.op("dve", lambda e, h=h, psc=psc: e.tensor_tensor(
                    out=Pm[:, h, :, :], in0=psc[:].rearrange("p (j t) -> p j t", j=4),
                    in1=mask_b.unsqueeze(1).to_broadcast([128, 4, 128]), op=ALU.mult),
                    r=[psc_b, cbf_b], w=[Pm_b[h]])
            yield
            for h in range(2):
                hs = slice(64 * h, 64 * h + 64)
                for j in range(4):
                    js = slice(128 * j, 128 * j + 128)
                    mm(po[c][hs, js], vtok[:, j, 128 * c + 64 * h:128 * c + 64 * h + 64], Pm[:, h, j, :],
                       j == 0, False, [vtok_b[j], Pm_b[h]], [po_b[c]], j == 3, skip=True)
            eGs.append((eGl[:, c, :], eGl_b[c], qg, qgb))
        for ch in range(8):
            j, hf = ch // 2, ch % 2
            rs_ = slice(64 * hf, 64 * hf + 64)
            cs = slice(64 * ch, 64 * ch + 64)
            pSb, pSb_b = accM.get()
            for c in range(3):
                for h in range(2):
                    hs = slice(64 * h, 64 * h + 64)
                    mm(pSb[hs, 64 * c:64 * c + 64], kdT[rs_, c, j, hs], vtok[rs_, j, 128 * c + 64 * h:128 * c + 64 * h + 64],
                       True, True, [kdT_b[c], vtok_b[j]], [pSb_b], (h == 1 and c == 2))
            for c in range(3):
                eG, eGb, qg, qgb = eGs[c]
                for h in range(2):
                    hs = slice(64 * h, 64 * h + 64)
                    mm(po[c][hs, cs], Sbf[hs, c, :], qg[hs, cs], False, True, [Sbf_b[c], qgb], [po_b[c]], h == 1, skip=True)
            for c in range(3):
                eG, eGb, qg, qgb = eGs[c]
                S.op("dve", lambda e, c=c, eG=eG, ch=ch, pSb=pSb: e.scalar_tensor_tensor(
                    out=Sst[:, c, :], in0=Sst[:, c, :], scalar=eG[:, ch:ch + 1], in1=pSb[:, 64 * c:64 * c + 64],
                    op0=ALU.mult, op1=ALU.add), r=[Sst_b[c], eGb, pSb_b], w=[Sst_b[c]])
                S.op("dve", lambda e, c=c: e.tensor_copy(out=Sbf[:, c, :], in_=Sst[:, c, :]),
                     r=[Sst_b[c]], w=[Sbf_b[c]])
            yield
        for c in range(3):
            o, ob = t32M.get()
            S.op("act", lambda e, o=o, c=c: e.activation(out=o[:], in_=po[c][:], func=AF.Copy), r=[po_b[c]], w=[ob])
            tq, tqb = t16M.get()
            S.op("act", lambda e, tq=tq, o=o: e.activation(out=tq[:, 0:TT], in_=o[:], func=AF.Square), r=[ob], w=[tqb])
            yield
            pst, pst_b = accM.get()
            mm(pst[:], ones64_b, tq[:, 0:TT], True, True, [tqb, cbf_b], [pst_b], True)
            rs, rsb = rstd_from_psum(pst, pst_b, t32M)
            yield
            S.op("dve", lambda e, o=o, rs=rs, c=c: e.scalar_tensor_tensor(
                out=o[:], in0=o[:], scalar=pc(l, "g_out", 3 + c), in1=rs[:], op0=ALU.mult, op1=ALU.mult),
                r=[ob, rsb, par_b[l]], w=[ob])
            S.op("dve", lambda e, o=o, c=c: e.tensor_tensor(out=hy[:, 3 + c, :], in0=o[:], in1=sgg[:, c, :], op=ALU.mult),
                 r=[ob, sgg_b[c]], w=[hy_b[3 + c]])
            yield

        for n in range(KC):
            if n % 2 == 0:
                wt, wtb = w_next(l, G_OUT(n // 2))
            wv = wt[:, 0:2048].rearrange("p (k n) -> p k n", k=8)
            nn = n % 2
            pst, pst_b = accM.get()
            for k in range(KC):
                mm(pst[:], wv[:, k, 128 * nn:128 * nn + 128], hy[:, k, :], k == 0, k == KC - 1,
                   [hy_b[k], wtb], [pst_b], k == KC - 1)
            xs = x_res[:, n, tsl]
            S.op("dve", lambda e, xs=xs, pst=pst: e.tensor_tensor(out=xs, in0=xs, in1=pst[:], op=ALU.add),
                 r=[pst_b, xb_[n][i]], w=[xb_[n][i]])
            if n % 2 == 1:
                yield

    def stage_F(l, i):
        tsl = slice(TT * i, TT * i + TT)
        hy, hy_b = hyF, hyF_b
        if i == 0:
            S.op("dve", lambda e: e.memset(fcar[:], 0.0), w=fcar_b)
        rmsnorm(l, i, "g_ffn", hy, hy_b, accF, t32F, t16F)
        yield
        for j in range(FC):
            wt, wtb = w_next(l, G_UP(j))
            conv = []
            for half in range(2):
                ci = half * FC + j
                wv = wt[:, 1024 * half:1024 * half + 1024].rearrange("p (k n) -> p k n", k=8)
                pst, pst_b = accF.get()
                for k in range(KC):
                    mm(pst[:], wv[:, k, :], hy[:, k, :], k == 0, k == KC - 1, [hy_b[k], wtb], [pst_b], k == KC - 1)
                pre, preb = t16F.get()
                S.op("act", lambda e, pre=pre, ci=ci: e.activation(out=pre[:, 0:2], in_=fcar[:, ci, :], func=AF.Copy),
                     r=[fcar_b[ci]], w=[preb])
                S.op("act", lambda e, pre=pre, pst=pst: e.activation(out=pre[:, 2:TT + 2], in_=pst[:], func=AF.Copy),
                     r=[pst_b], w=[preb])
                S.op("act", lambda e, pre=pre, ci=ci: e.activation(out=fcar[:, ci, :], in_=pre[:, TT:TT + 2], func=AF.Copy),
                     r=[preb], w=[fcar_b[ci]])
                cv, cvb = t16F.get()
                S.op("act", lambda e, pst=pst, cv=cv, ci=ci: e.activation(
                    out=cv[:, 0:TT], in_=pst[:], func=AF.Copy, scale=pc(l, "ffn_cw", 44 * 2 + ci)),
                    r=[pst_b, par_b[l]], w=[cvb])
                for tap in (1, 0):
                    S.op("dve", lambda e, pre=pre, cv=cv, ci=ci, tap=tap: e.scalar_tensor_tensor(
                        out=cv[:, 0:TT], in0=pre[:, tap:tap + TT], scalar=pc(l, "ffn_cw", 44 * tap + ci),
                        in1=cv[:, 0:TT], op0=ALU.mult, op1=ALU.add), r=[preb, cvb, par_b[l]], w=[cvb])
                conv.append((cv, cvb))
            (gc, gcb), (uc, ucb) = conv
            sgt, sgtb = t16F.get()
            S.op("act", lambda e, gc=gc, sgt=sgt: e.activation(out=sgt[:, 0:TT], in_=gc[:, 0:TT], func=AF.Silu),
                 r=[gcb], w=[sgtb])
            S.op("dve", lambda e, sgt=sgt, uc=uc, j=j: e.tensor_tensor(out=a_buf[:, j, :], in0=sgt[:, 0:TT],
                                                                        in1=uc[:, 0:TT], op=ALU.mult),
                 r=[sgtb, ucb], w=[a_b[j]])
            yield
        for n in range(KC):
            pst, pst_b = accF.get()
            for hf in range(2):
                wt, wtb = w_next(l, G_DN(n, hf))
                wv = wt[:, 0:1408].rearrange("p (k n) -> p k n", k=11)
                for k in range(11):
                    j = 11 * hf + k
                    mm(pst[:], wv[:, k, :], a_buf[:, j, :], j == 0, j == FC - 1, [a_b[j], wtb], [pst_b], j == FC - 1)
            xs = x_res[:, n, tsl]
            S.op("dve", lambda e, xs=xs, pst=pst: e.tensor_tensor(out=xs, in0=xs, in1=pst[:], op=ALU.add),
                 r=[pst_b, xb_[n][i]], w=[xb_[n][i]])
            yield
        s = i % 2
        S.dma(f"io{s}", lambda e: e.dma_start(
            out=io[s][:].rearrange("p (b j) -> p b j", b=4),
            in_=p_d[l][TT * i:TT * i + TT, :].rearrange("(b t) j -> t b j", t=128)), w=[io_b[s]])
        for jb in range(2):
            pst, pst_b = accF.get()
            for blk in range(4):
                S.op("pe", lambda e, pst=pst, blk=blk, jb=jb: e.transpose(
                    pst[:, 128 * blk:128 * blk + 128], io[s][:, 256 * blk + 128 * jb:256 * blk + 128 * jb + 128], ident_f),
                    r=[io_b[s], cst_b], w=[pst_b], signal=(blk == 3))
            S.op("act", lambda e, pst=pst, jb=jb: e.activation(out=pT[:, jb, :], in_=pst[:], func=AF.Copy),
                 r=[pst_b], w=[pT_b[jb]])
        for k in range(KC):
            xs = x_res[:, k, tsl]
            S.op("act", lambda e, k=k, xs=xs: e.activation(out=hy[:, k, :], in_=xs, func=AF.Copy),
                 r=[xb_[k][i]], w=[hy_b[k]])
        yield
        for n in range(KC):
            if n % 2 == 0:
                wt, wtb = w_next(l, G_PG(n // 2))
            wv = wt[:, 0:2048].rearrange("p (k n) -> p k n", k=8)
            nn = n % 2
            pst, pst_b = accF.get()
            for k in range(KC):
                mm(pst[:], wv[:, k, 128 * nn:128 * nn + 128], hy[:, k, :], k == 0, k == KC - 1,
                   [hy_b[k], wtb], [pst_b], k == KC - 1)
            S.op("act", lambda e, pst=pst, n=n: e.activation(out=a_buf[:, n, :], in_=pst[:], func=AF.Sigmoid),
                 r=[pst_b], w=[a_b[n]])
            if n % 2 == 1:
                yield
        wt, wtb = w_next(l, G_PP)
        wv = wt[:, 0:2048].rearrange("p (k n) -> p k n", k=2)
        for n in range(KC):
            pst, pst_b = accF.get()
            for k in range(2):
                mm(pst[:], wv[:, k, 128 * n:128 * n + 128], pT[:, k, :], k == 0, k == 1,
                   [pT_b[k], wtb], [pst_b], k == 1)
            tm, tmb = t32F.get()
            S.op("dve", lambda e, tm=tm, pst=pst, n=n: e.tensor_tensor(out=tm[:], in0=a_buf[:, n, :], in1=pst[:], op=ALU.mult),
                 r=[pst_b, a_b[n]], w=[tmb])
            xs = x_res[:, n, tsl]
            S.op("dve", lambda e, xs=xs, tm=tm: e.tensor_tensor(out=xs, in0=xs, in1=tm[:], op=ALU.add),
                 r=[tmb, xb_[n][i]], w=[xb_[n][i]])
        yield

    MY, FY = 1, 1

    def interleave(ga, gb):
        pa = pb = 0
        while ga is not None or gb is not None:
            if gb is None or (ga is not None and pa * FY <= pb * MY):
                try:
                    next(ga)
                    pa += 1
                except StopIteration:
                    ga = None
            else:
                try:
                    next(gb)
                    pb += 1
                except StopIteration:
                    gb = None

    def final_out(tiles_):
        for i in tiles_:
            if final_norm:
                pst, pst_b = accM.get()
                for k in range(KC):
                    tq, tqb = t16M.get()
                    xs = x_res[:, k, TT * i:TT * i + TT]
                    S.op("act", lambda e, tq=tq, xs=xs: e.activation(out=tq[:, 0:TT], in_=xs, func=AF.Square),
                         r=[xb_[k][i]], w=[tqb])
                    mm(pst[:], onesD[:], tq[:, 0:TT], k == 0, k == KC - 1, [tqb, cbf_b], [pst_b], True)
                rs, rsb = rstd_from_psum(pst, pst_b, t32M)
                for k in range(KC):
                    xs = x_res[:, k, TT * i:TT * i + TT]
                    S.op("dve", lambda e, k=k, xs=xs, rs=rs: e.scalar_tensor_tensor(
                        out=xs, in0=xs, scalar=pc(DEPTH - 1, "g_fin", k), in1=rs[:], op0=ALU.mult, op1=ALU.mult),
                        r=[xb_[k][i], rsb, par_b[DEPTH - 1]], w=[xb_[k][i]])
                yield
            for blk in range(4):
                tb = 4 * i + blk
                s = tb % 2
                for half in range(2):
                    pst, pst_b = accM.get()
                    for kk in range(4):
                        k = 4 * half + kk
                        S.op("pe", lambda e, pst=pst, kk=kk, k=k, tb=tb: e.transpose(
                            pst[:, 128 * kk:128 * kk + 128], x_res[:, k, 128 * tb:128 * tb + 128], ident_f),
                            r=[xb_[k][i], cst_b], w=[pst_b], signal=(kk == 3))
                    if half == 0:
                        S.op("act", lambda e, pst=pst, s=s: e.activation(out=io[s][:, 0:512], in_=pst[:], func=AF.Copy),
                             r=[pst_b], w=[io_b[s]])
                    else:
                        S.op("dve", lambda e, pst=pst, s=s: e.tensor_copy(out=io[s][:, 512:1024], in_=pst[:]),
                             r=[pst_b], w=[io_b[s]])
                S.dma(f"outd{s}", lambda e, s=s, tb=tb: e.dma_start(out=out_d[128 * tb:128 * tb + 128, :], in_=io[s][:]),
                      r=[io_b[s]])
                yield

    tiles = [(l, i) for l in range(n_layers) for i in range(NT)]
    gf_next = None
    for k in range(len(tiles) + 1):
        gm = stage_M(*tiles[k]) if k < len(tiles) else None
        gf = gf_next
        if k == 0:
            gf = load_x(range(4, T // 128))
        if k == len(tiles):
            FINAL_GEN = final_out(range(0, NT - 1))
            gm = FINAL_GEN
        interleave(gm, gf)
        if k < len(tiles):
            gf_next = stage_F(*tiles[k])
            next(gf_next)
    for _ in final_out([NT - 1]):
        pass

    nc.sync.wait_ge(S.sems["outd0"], S.cnt["outd0"])
    nc.sync.wait_ge(S.sems["outd1"], S.cnt["outd1"])
    es.close()
    return nc, rec


def build_program(**kw):
    _, order = _build(None, **kw)
    nc, _ = _build(order, **kw)
    return nc


def _chunks(v, n):
    return np.ascontiguousarray(np.asarray(v, np.float32).reshape(n, 128).T)


def _pack_params(inp, l):
    par = np.zeros((128, NPAR), np.float32)

    def put(name, arr):
        o, c = _PAR[name]
        assert arr.shape == (128, c), (name, arr.shape)
        par[:, o:o + c] = arr

    put("g_mix", _chunks(inp["norm_mix_gain"][l], 8))
    put("g_ffn", _chunks(inp["norm_ffn_gain"][l], 8))
    put("g_out", _chunks(inp["mix_out_gain"][l], 8))
    put("rg_cw", np.concatenate([_chunks(inp["rg_conv_w"][l][t], 3) for t in range(4)], axis=1))
    put("rg_cb", _chunks(inp["rg_conv_b"][l], 3))
    put("rg_br", _chunks(np.asarray(inp["rg_b_r"][l]).reshape(-1), 3))
    put("rg_bi", _chunks(np.asarray(inp["rg_b_i"][l]).reshape(-1), 3))
    put("rg_lam", _chunks(inp["rg_lambda"][l], 3))
    put("hg_lb0", _chunks(inp["hg_lower_bounds"][0], 3))
    put("hg_lb1", _chunks(inp["hg_lower_bounds"][min(1, DEPTH - 1)], 3))
    put("sc_cw", np.concatenate([_chunks(inp["sc_conv_w"][l][t], 2) for t in range(3)], axis=1))
    put("ffn_cw", np.concatenate([_chunks(inp["ffn_conv_w"][l][t], 44) for t in range(3)], axis=1))
    put("g_fin", _chunks(inp["final_norm_gain"], 8))
    return par


def _pack_gates(inp, l):
    out = np.zeros((128, 768), np.float32)
    for gi, name in enumerate(("rg_w_r", "rg_w_i")):
        w = np.asarray(inp[name][l], np.float32)
        for c in range(3):
            for h in range(2):
                out[64 * h:64 * h + 64, 384 * gi + 128 * c + 64 * h:384 * gi + 128 * c + 64 * h + 64] = w[2 * c + h]
    return out


def _consts():
    c = np.zeros((128, 896), np.float32)
    c[:, 0:128] = np.eye(128, dtype=np.float32)
    for h in range(2):
        c[64 * h:64 * h + 64, 128 + 64 * h:128 + 64 * h + 64] = 1.0 / 64.0
    s = np.arange(128)[:, None]
    t = np.arange(128)[None, :]
    c[:, 256:384] = ((s // 64 == t // 64) & (s <= t)).astype(np.float32)
    m = np.ones((128, 512), np.float32)
    m[:, ::64] = 0.0
    c[:, 384:896] = m
    return c


def make_in_maps(inp):
    inp = {k: np.asarray(v) for k, v in inp.items()}
    B = inp["x"].shape[0]
    shared = {
        "par": np.stack([_pack_params(inp, l) for l in range(DEPTH)]),
        "cst": _consts(),
        "w_in": np.ascontiguousarray(inp["w_in"], np.float32),
        "w_gates": np.stack([_pack_gates(inp, l) for l in range(DEPTH)]),
        "w_out": np.ascontiguousarray(inp["w_out"], np.float32),
        "w_up": np.ascontiguousarray(inp["ffn_w_up"], np.float32),
        "w_down": np.ascontiguousarray(inp["ffn_w_down"], np.float32),
        "w_pgate": np.ascontiguousarray(inp["ple_w_gate"], np.float32),
        "w_pproj": np.ascontiguousarray(inp["ple_w_proj"], np.float32),
    }
    maps = []
    for b in range(B):
        m = dict(shared)
        m["x"] = np.ascontiguousarray(inp["x"][b], np.float32)
        m["p"] = np.ascontiguousarray(inp["p"][:, b], np.float32)
        maps.append(m)
    return maps


def kernel(**inputs):
    nc = build_program()
    maps = make_in_maps(inputs)
    res = run_bass_kernel_spmd(nc, maps, core_ids=list(range(len(maps))))
    return np.stack([np.asarray(r["out"], np.float32) for r in res.results], axis=0)
```
